# Optimizing a Trainium2 kernel written in Bass

```python
import math
import jax, jax.numpy as jnp
from jax import lax
import numpy as np

D_MODEL = 1024
BATCH = 8
SEQ = 2048
DEPTH = 2

N_MIXERS = 2
N_HEADS = 16
HEAD_DIM = 64
ATTN_WIDTH = N_HEADS * HEAD_DIM
FOX_BLOCK = 128
FOX_IN = 4 * ATTN_WIDTH + N_HEADS
NSA_GROUPS = 4
NSA_REP = N_HEADS // NSA_GROUPS
KV_WIDTH = NSA_GROUPS * HEAD_DIM
CMP_LEN = 32
CMP_STRIDE = 16
CMP_HIDDEN = 256
SEL_LEN = 64
N_SELECT = 8
SEL_QCHUNK = 32
WINDOW = 512
WIN_BLOCK = 128
N_BRANCH = 3
NSA_IN = 2 * ATTN_WIDTH + 6 * KV_WIDTH + N_BRANCH * N_HEADS
ROPE_THETA = 500000.0
ROPE_DIM = HEAD_DIM // 4
NORM_EPS = 1e-6
NEG = -1e30
FORCE = 1e6
N_FOX = (DEPTH + 1) // 2
N_NSA = DEPTH // 2

kernel_name = "fox_nsa_interleaved_hybrid"


def rmsnorm(x, g):
    xf = x.astype(jnp.float32)
    y = xf * lax.rsqrt(jnp.mean(xf * xf, axis=-1, keepdims=True) + NORM_EPS)
    return (y * g.astype(jnp.float32)).astype(x.dtype)


def rope_partial(x, pos):
    half = ROPE_DIM // 2
    inv_freq = jnp.power(ROPE_THETA, -jnp.arange(half, dtype=jnp.float32) * (2.0 / ROPE_DIM))
    ang = pos.astype(jnp.float32)[:, None] * inv_freq[None, :]
    shape = (pos.shape[0],) + (1,) * (x.ndim - 3) + (half,)
    cos = jnp.cos(ang).reshape(shape)
    sin = jnp.sin(ang).reshape(shape)
    xr = x[..., :ROPE_DIM].astype(jnp.float32)
    x1, x2 = xr[..., :half], xr[..., half:]
    rot = jnp.concatenate([x1 * cos - x2 * sin, x1 * sin + x2 * cos], axis=-1).astype(x.dtype)
    return jnp.concatenate([rot, x[..., ROPE_DIM:]], axis=-1)


def fox_mixer(h, w_in, b_f, w_out):
    B, S, _ = h.shape
    W = ATTN_WIDTH
    proj = h @ w_in
    q, k, v, f, z = jnp.split(proj, [W, 2 * W, 3 * W, 3 * W + N_HEADS], axis=-1)
    q = q.reshape(B, S, N_HEADS, HEAD_DIM)
    k = k.reshape(B, S, N_HEADS, HEAD_DIM)
    v = v.reshape(B, S, N_HEADS, HEAD_DIM)
    log_f = jax.nn.log_sigmoid(f.astype(jnp.float32) + b_f.astype(jnp.float32))
    c = jnp.cumsum(log_f, axis=1).transpose(0, 2, 1)
    scale = HEAD_DIM ** -0.5
    outs = []
    for qb in range(S // FOX_BLOCK):
        s0, s1 = qb * FOX_BLOCK, (qb + 1) * FOX_BLOCK
        logits = jnp.einsum('bqhd,bkhd->bhqk', q[:, s0:s1], k[:, :s1]).astype(jnp.float32) * scale
        logits = logits + c[:, :, s0:s1, None] - c[:, :, None, :s1]
        tq = jnp.arange(s0, s1)
        tk = jnp.arange(s1)
        logits = jnp.where(tk[None, :] <= tq[:, None], logits, NEG)
        p = jax.nn.softmax(logits, axis=-1).astype(v.dtype)
        outs.append(jnp.einsum('bhqk,bkhd->bqhd', p, v[:, :s1]))
    o = jnp.concatenate(outs, axis=1).reshape(B, S, W)
    return (o * jax.nn.silu(z)) @ w_out


def compress_blocks(x, pe, w1, w2):
    B, S, G, D = x.shape
    n_chunk = S // CMP_STRIDE
    r = CMP_LEN // CMP_STRIDE
    n_cmp = n_chunk - r + 1
    ch = x.reshape(B, n_chunk, CMP_STRIDE, G, D)
    blk = jnp.concatenate([ch[:, j:j + n_cmp] for j in range(r)], axis=2)
    blk = blk + pe[None, None, :, None, :]
    blk = blk.transpose(0, 1, 3, 2, 4).reshape(B, n_cmp, G, CMP_LEN * D)
    return jax.nn.silu(blk @ w1) @ w2


def nsa_mixer(h, w_in, pe_k, w_ck1, w_ck2, pe_v, w_cv1, w_cv2, w_out):
    B, S, _ = h.shape
    G, R, D, W = NSA_GROUPS, NSA_REP, HEAD_DIM, ATTN_WIDTH
    proj = h @ w_in
    offs = [W + i * KV_WIDTH for i in range(1, 7)] + [W + 6 * KV_WIDTH + N_BRANCH * N_HEADS]
    q, kc_raw, vc_raw, ks_raw, vs_raw, kw_raw, vw_raw, gate_raw, z = jnp.split(proj, [W] + offs, axis=-1)
    pos = jnp.arange(S)
    q = rope_partial(q.reshape(B, S, G, R, D), pos)
    scale = D ** -0.5

    kc = compress_blocks(kc_raw.reshape(B, S, G, D), pe_k, w_ck1, w_ck2)
    vc = compress_blocks(vc_raw.reshape(B, S, G, D), pe_v, w_cv1, w_cv2)
    n_cmp = kc.shape[1]
    cmp_end = jnp.arange(n_cmp) * CMP_STRIDE + CMP_LEN - 1
    kc = rope_partial(kc, cmp_end)
    lg_c = jnp.einsum('bsgrd,bcgd->bgrsc', q, kc).astype(jnp.float32) * scale
    mask_c = cmp_end[None, :] <= pos[:, None]
    p_c = jax.nn.softmax(jnp.where(mask_c, lg_c, NEG), axis=-1)
    p_c = jnp.where(mask_c, p_c, 0.0)
    o_c = jnp.einsum('bgrsc,bcgd->bsgrd', p_c.astype(vc.dtype), vc)

    n_sel_blk = S // SEL_LEN
    n_sel = min(N_SELECT, n_sel_blk)
    ci = jnp.arange(n_cmp) * CMP_STRIDE
    sj = jnp.arange(n_sel_blk) * SEL_LEN
    overlap = ((ci[:, None] < sj[None, :] + SEL_LEN) & (ci[:, None] + CMP_LEN > sj[None, :])).astype(jnp.float32)
    imp = jnp.einsum('bgsc,cj->bgsj', p_c.sum(axis=2), overlap)
    cur = pos // SEL_LEN
    blk_ids = jnp.arange(n_sel_blk)
    forced = (blk_ids[None, :] == 0) | (blk_ids[None, :] == cur[:, None]) | (blk_ids[None, :] == cur[:, None] - 1)
    causal = blk_ids[None, :] <= cur[:, None]
    imp = jnp.where(forced, FORCE, jnp.where(causal, imp, -1.0))
    _, idx = lax.top_k(imp, n_sel)

    ks = rope_partial(ks_raw.reshape(B, S, G, D), pos)
    vs = vs_raw.reshape(B, S, G, D)
    k_blocks = ks.reshape(B, n_sel_blk, SEL_LEN, G, D).transpose(0, 3, 1, 2, 4)
    v_blocks = vs.reshape(B, n_sel_blk, SEL_LEN, G, D).transpose(0, 3, 1, 2, 4)
    n_qc = S // SEL_QCHUNK
    q_ch = q.reshape(B, n_qc, SEL_QCHUNK, G, R, D).transpose(1, 0, 2, 3, 4, 5)
    idx_ch = idx.transpose(0, 2, 1, 3).reshape(B, n_qc, SEL_QCHUNK, G, n_sel).transpose(1, 0, 2, 3, 4)
    pos_ch = pos.reshape(n_qc, SEL_QCHUNK)
    bi = jnp.arange(B)[:, None, None, None]
    gi = jnp.arange(G)[None, None, :, None]
    l_off = jnp.arange(SEL_LEN)

    def sel_chunk(args):
        qc, ic, tc = args
        kb = k_blocks[bi, gi, ic]
        vb = v_blocks[bi, gi, ic]
        lg = jnp.einsum('btgrd,btgnld->btgrnl', qc, kb).astype(jnp.float32) * scale
        kpos = ic[..., None] * SEL_LEN + l_off
        m = (kpos <= tc[None, :, None, None, None])[:, :, :, None]
        lg = jnp.where(m, lg, NEG)
        p = jax.nn.softmax(lg.reshape(B, SEL_QCHUNK, G, R, n_sel * SEL_LEN), axis=-1)
        p = p.reshape(B, SEL_QCHUNK, G, R, n_sel, SEL_LEN).astype(vb.dtype)
        return jnp.einsum('btgrnl,btgnld->btgrd', p, vb)

    o_s = lax.map(sel_chunk, (q_ch, idx_ch, pos_ch))
    o_s = o_s.transpose(1, 0, 2, 3, 4, 5).reshape(B, S, G, R, D)

    kw = rope_partial(kw_raw.reshape(B, S, G, D), pos)
    vw = vw_raw.reshape(B, S, G, D)
    n_qb = S // WIN_BLOCK
    n_wb = -(-WINDOW // WIN_BLOCK)
    pad = n_wb * WIN_BLOCK
    band = (n_wb + 1) * WIN_BLOCK
    kpad = jnp.pad(kw, ((0, 0), (pad, 0), (0, 0), (0, 0))).reshape(B, n_qb + n_wb, WIN_BLOCK, G, D)
    vpad = jnp.pad(vw, ((0, 0), (pad, 0), (0, 0), (0, 0))).reshape(B, n_qb + n_wb, WIN_BLOCK, G, D)
    k_band = jnp.concatenate([kpad[:, j:j + n_qb] for j in range(n_wb + 1)], axis=2)
    v_band = jnp.concatenate([vpad[:, j:j + n_qb] for j in range(n_wb + 1)], axis=2)
    qb = q.reshape(B, n_qb, WIN_BLOCK, G, R, D)
    lg_w = jnp.einsum('bnqgrd,bnkgd->bngrqk', qb, k_band).astype(jnp.float32) * scale
    tq = pos.reshape(n_qb, WIN_BLOCK)
    tk = jnp.arange(n_qb)[:, None] * WIN_BLOCK - pad + jnp.arange(band)[None, :]
    m_w = (tk[:, None, :] <= tq[:, :, None]) & (tk[:, None, :] > tq[:, :, None] - WINDOW) & (tk[:, None, :] >= 0)
    lg_w = jnp.where(m_w[None, :, None, None], lg_w, NEG)
    p_w = jax.nn.softmax(lg_w, axis=-1).astype(v_band.dtype)
    o_w = jnp.einsum('bngrqk,bnkgd->bnqgrd', p_w, v_band).reshape(B, S, G, R, D)

    g = jax.nn.sigmoid(gate_raw.astype(jnp.float32)).reshape(B, S, G, R, N_BRANCH).astype(o_c.dtype)
    o = g[..., 0:1] * o_c + g[..., 1:2] * o_s + g[..., 2:3] * o_w
    return (o.reshape(B, S, W) * jax.nn.silu(z)) @ w_out


def setup_inputs(seed: int = 0) -> dict:
    key = jax.random.key(seed)
    ks = jax.random.split(key, 16)
    f32 = jnp.float32
    W = ATTN_WIDTH
    cdim = CMP_LEN * HEAD_DIM
    return {
        "x": jax.random.normal(ks[0], (BATCH, SEQ, D_MODEL), f32),
        "norm_g": 1.0 + 0.1 * jax.random.normal(ks[1], (DEPTH, D_MODEL), f32),
        "fox_w_in": jax.random.normal(ks[2], (N_FOX, D_MODEL, FOX_IN), f32) * D_MODEL ** -0.5,
        "fox_b_f": jax.random.uniform(ks[3], (N_FOX, N_HEADS), f32, 1.0, 6.0),
        "fox_w_out": jax.random.normal(ks[4], (N_FOX, W, D_MODEL), f32) * W ** -0.5,
        "nsa_w_in": jax.random.normal(ks[5], (N_NSA, D_MODEL, NSA_IN), f32) * D_MODEL ** -0.5,
        "nsa_pe_k": 0.1 * jax.random.normal(ks[6], (N_NSA, CMP_LEN, HEAD_DIM), f32),
        "nsa_w_ck1": jax.random.normal(ks[7], (N_NSA, cdim, CMP_HIDDEN), f32) * cdim ** -0.5,
        "nsa_w_ck2": jax.random.normal(ks[8], (N_NSA, CMP_HIDDEN, HEAD_DIM), f32) * CMP_HIDDEN ** -0.5,
        "nsa_pe_v": 0.1 * jax.random.normal(ks[9], (N_NSA, CMP_LEN, HEAD_DIM), f32),
        "nsa_w_cv1": jax.random.normal(ks[10], (N_NSA, cdim, CMP_HIDDEN), f32) * cdim ** -0.5,
        "nsa_w_cv2": jax.random.normal(ks[11], (N_NSA, CMP_HIDDEN, HEAD_DIM), f32) * CMP_HIDDEN ** -0.5,
        "nsa_w_out": jax.random.normal(ks[12], (N_NSA, W, D_MODEL), f32) * W ** -0.5,
        "final_g": 1.0 + 0.1 * jax.random.normal(ks[13], (D_MODEL,), f32),
    }


def reference(x, norm_g, fox_w_in, fox_b_f, fox_w_out, nsa_w_in, nsa_pe_k, nsa_w_ck1, nsa_w_ck2,
              nsa_pe_v, nsa_w_cv1, nsa_w_cv2, nsa_w_out, final_g):
    for i in range(DEPTH):
        h = rmsnorm(x, norm_g[i])
        j = i // N_MIXERS
        if i % N_MIXERS == 0:
            y = fox_mixer(h, fox_w_in[j], fox_b_f[j], fox_w_out[j])
        else:
            y = nsa_mixer(h, nsa_w_in[j], nsa_pe_k[j], nsa_w_ck1[j], nsa_w_ck2[j],
                          nsa_pe_v[j], nsa_w_cv1[j], nsa_w_cv2[j], nsa_w_out[j])
        x = x + y
    return rmsnorm(x, final_g)
```

```python
import numpy as np
import ml_dtypes
from contextlib import ExitStack
import concourse.bass as bass
import concourse.mybir as mybir
from concourse.bass_utils import run_bass_kernel_spmd

F32 = mybir.dt.float32
BF = mybir.dt.bfloat16
AF = mybir.ActivationFunctionType
ALU = mybir.AluOpType

S = 2048
D = 1024
NT = 16
NEGM = -30000.0
EPS = 1e-6
FOX_IN = 4112
NSA_IN = 3632

ENGS = ["tensor", "vector", "scalar", "gpsimd", "sync"]


class _Rec:
    def __init__(self):
        self.call = None

    def __getattr__(self, name):
        def f(*a, **k):
            self.call = (name, a, k)
            return self
        return f


class Prog:
    def __init__(self, nc):
        self.nc = nc
        self.ops = {e: [] for e in ENGS}
        self.res = {}
        self.dma_cnt = {}
        self.final_tokens = []
        self.const_names = []

    def _st(self, r):
        st = self.res.get(r)
        if st is None:
            st = {"w": None, "r": {}}
            self.res[r] = st
        return st

    def op(self, eng, fn, reads=(), writes=(), dma=None, final=False, grp=None):
        rec = _Rec()
        fn(rec)
        assert rec.call is not None
        deps = []
        for r in reads:
            st = self._st(r)
            if st["w"] is not None:
                deps.append(st["w"])
            if isinstance(r, tuple) and r[0] in ("pA", "pT", "pS", "pO"):
                for (re_, _), rk in st["r"].items():
                    if re_ != eng:
                        deps.append(rk)
        for w in writes:
            st = self._st(w)
            if st["w"] is not None:
                deps.append(st["w"])
            deps.extend(st["r"].values())
        idx = len(self.ops[eng])
        o = {"call": rec.call, "deps": [], "signal": False, "dma": dma, "grp": grp}
        me = (eng, idx)
        for d in deps:
            if d == me:
                continue
            dop = self.ops[d[0]][d[1]]
            if grp is not None and dop["grp"] == grp:
                continue
            if dop["dma"] is None and d[0] == "tensor" and eng == "tensor":
                continue
            if d not in o["deps"]:
                o["deps"].append(d)
                dop["signal"] = True
        if dma is not None:
            o["signal"] = True
        self.ops[eng].append(o)
        for r in reads:
            self._st(r)["r"][(eng, dma)] = me
        for w in writes:
            st = self._st(w)
            st["w"] = me
            st["r"] = {}
        if final:
            self.final_tokens.append(me)
        return me

    def emit(self):
        nc = self.nc
        dma_keys = []
        for e in ENGS:
            cnt = 0
            for o in self.ops[e]:
                if o["dma"] is not None:
                    k = o["dma"]
                    if k not in self.dma_cnt:
                        self.dma_cnt[k] = 0
                        dma_keys.append(k)
                    self.dma_cnt[k] += 16
                    o["tok"] = (("dma", k), self.dma_cnt[k])
                elif o["signal"]:
                    cnt += 1
                    o["tok"] = (("eng", e), cnt)
        with ExitStack() as es:
            sems = {}
            for e in ENGS:
                sems[("eng", e)] = es.enter_context(nc.semaphore("s_" + e))
            for i, k in enumerate(dma_keys):
                sems[("dma", k)] = es.enter_context(nc.semaphore("d%d" % i))
            block = es.enter_context(nc.Block())
            prog = self

            def body(ename):
                def f(engobj):
                    waited = {}

                    def wait_for(d):
                        sk, val = prog.ops[d[0]][d[1]]["tok"]
                        if waited.get(sk, 0) >= val:
                            return
                        waited[sk] = val
                        engobj.wait_ge(sems[sk], val)

                    for o in prog.ops[ename]:
                        for d in o["deps"]:
                            wait_for(d)
                        name, a, k = o["call"]
                        ins = getattr(engobj, name)(*a, **k)
                        if o["signal"]:
                            sk, val = o["tok"]
                            ins.then_inc(sems[sk], 16 if o["dma"] is not None else 1)
                    if ename == "sync":
                        for d in prog.final_tokens:
                            wait_for(d)
                return f

            block.tensor(body("tensor"))
            block.vector(body("vector"))
            block.scalar(body("scalar"))
            block.gpsimd(body("gpsimd"))
            block.sync(body("sync"))


def host_constants():
    bf = ml_dtypes.bfloat16
    c = {}
    c["c_ident"] = np.eye(128, dtype=np.float32).astype(bf)
    kk = np.arange(128)[:, None]
    qq = np.arange(128)[None, :]
    c["c_tri"] = (kk <= qq).astype(np.float32)
    c["c_ones"] = np.ones((128, 128), np.float32)
    c["c_cm"] = np.where(kk <= qq, 0.0, NEGM).astype(np.float32).astype(bf)
    c["c_am"] = np.where(kk > qq, 0.0, NEGM).astype(np.float32).astype(bf)
    cc = np.arange(128)[:, None]
    tt = np.arange(S)[None, :]
    c["c_cmpmask"] = np.where((cc < 127) & (16 * cc + 31 <= tt), 0.0, NEGM).astype(np.float32).astype(bf)
    inv_freq = np.power(np.float32(500000.0), -np.arange(8, dtype=np.float32) * np.float32(2.0 / 16)).astype(np.float32)
    pos = (np.arange(NT)[None, :] * 128 + np.arange(128)[:, None]).astype(np.float32)
    ang = (pos[:, :, None] * inv_freq[None, None, :]).astype(np.float32)
    cos = np.cos(ang).astype(np.float32)
    sin = np.sin(ang).astype(np.float32)
    sc = np.array([0.125] * 4 + [1.0, 1.0], np.float32)[None, None, :, None]
    c["c_cos6"] = np.ascontiguousarray((cos[:, :, None, :] * sc).astype(np.float32).reshape(128, NT * 48))
    c["c_sin6"] = np.ascontiguousarray((sin[:, :, None, :] * sc).astype(np.float32).reshape(128, NT * 48))
    cpos = (np.arange(128) * 16 + 31).astype(np.float32)
    cang = (cpos[:, None] * inv_freq[None, :]).astype(np.float32)
    c["c_ccos"] = np.cos(cang).astype(np.float32)
    c["c_csin"] = np.sin(cang).astype(np.float32)
    jj = np.arange(32)[:, None]
    c["c_eall"] = ((np.arange(S)[None, :] // 64) == jj).astype(np.float32).astype(bf)
    p = np.arange(128)[:, None, None]
    T = np.arange(NT)[None, :, None]
    j = np.arange(32)[None, None, :]
    cur = 2 * T + (p >= 64)
    dd = j - cur
    forced = (j == 0) | (dd == 0) | (dd == -1)
    causal = dd <= 0
    A = ((~forced) & causal).astype(np.float32)
    B = np.where(forced, 1e6, np.where(causal, 0.0, -1.0)).astype(np.float32)
    c["c_impA"] = np.ascontiguousarray(np.broadcast_to(A, (128, NT, 32)).reshape(128, NT * 32))
    c["c_impB"] = np.ascontiguousarray(np.broadcast_to(B, (128, NT, 32)).reshape(128, NT * 32))
    ci = np.arange(128)[:, None] * 16
    sj = np.arange(32)[None, :] * 64
    ovl = ((ci < sj + 64) & (ci + 32 > sj) & (np.arange(128)[:, None] < 127)).astype(np.float32)
    c["c_ovl"] = ovl.astype(bf)
    return c


CONST_SPECS = [
    ("c_ident", [128, 128], BF), ("c_tri", [128, 128], F32), ("c_ones", [128, 128], F32),
    ("c_cm", [128, 128], BF), ("c_am", [128, 128], BF), ("c_cmpmask", [128, S], BF),
    ("c_cos6", [128, NT * 48], F32), ("c_sin6", [128, NT * 48], F32),
    ("c_ccos", [128, 8], F32), ("c_csin", [128, 8], F32), ("c_eall", [32, S], BF),
    ("c_impA", [128, NT * 32], F32), ("c_impB", [128, NT * 32], F32), ("c_ovl", [128, 32], BF),
]

IN_SPECS = [
    ("x", [S, D]), ("norm_g", [2, D]), ("fox_w_in", [D, FOX_IN]), ("fox_b_f", [1, 16]),
    ("fox_w_out", [D, D]), ("nsa_w_in", [D, NSA_IN]), ("nsa_pe_k", [32, 64]), ("nsa_w_ck1", [2048, 256]),
    ("nsa_w_ck2", [256, 64]), ("nsa_pe_v", [32, 64]), ("nsa_w_cv1", [2048, 256]), ("nsa_w_cv2", [256, 64]),
    ("nsa_w_out", [D, D]), ("final_g", [1, D]),
]


class _Stop(Exception):
    pass


def _raise(msg):
    raise RuntimeError(msg)


def build(layers=(0, 1), debug=False, stop=None):
    nc = bass.Bass("TRN2", target_bir_lowering=False)
    din = {}
    for name, shape in IN_SPECS:
        din[name] = nc.dram_tensor(name, shape, F32, kind="ExternalInput").ap()
    for name, shape, dt in CONST_SPECS:
        din[name] = nc.dram_tensor(name, shape, dt, kind="ExternalInput").ap()
    out_d = nc.dram_tensor("out", [S, D], F32, kind="ExternalOutput").ap()
    x1_d = nc.dram_tensor("x1s", [S, D], F32, kind="ExternalOutput" if debug else "Internal").ap()
    og_d = nc.dram_tensor("ogs", [S, D], BF, kind="Internal").ap()

    es = ExitStack()
    with es:
        def sb(name, shape, dt):
            return es.enter_context(nc.sbuf_tensor(name, shape, dt))

        def ps(name, shape, dt):
            return es.enter_context(nc.psum_tensor(name, shape, dt))

        hT = sb("hT", [128, 8, S], BF)
        wbuf = [sb("wbuf%d" % i, [128, 8192], BF) for i in range(2)]
        xt = [sb("xt%d" % i, [128, D], F32) for i in range(2)]
        ht = [sb("ht%d" % i, [128, D], BF) for i in range(2)]
        gbc = sb("gbc", [128, D], F32)
        stageQ = sb("stageQ", [128, NT, 4, 96], BF)
        stg = [sb("stg%d" % i, [128, 4, 96], BF) for i in range(2)]
        QT = sb("QT", [128, 4, S], BF)
        KT = sb("KT", [128, 4, S], BF)
        Vaug = sb("Vaug", [128, NT, 4, 65], BF)
        zs = sb("zs", [128, NT, 256], BF)
        acc = sb("acc", [128, NT, 256], F32)
        G = sb("G", [128, NT, 12], F32)
        PT = [sb("PT%d" % i, [128, 512], BF) for i in range(4)]
        ident = sb("ident", [128, 128], BF)
        tri = sb("tri", [128, 128], F32)
        ones = sb("ones", [128, 128], F32)
        cm = sb("cm", [128, 128], BF)
        am = sb("am", [128, 128], BF)
        cmpmask = sb("cmpmask", [128, S], BF)
        cos6 = sb("cos6", [128, NT, 6, 8], F32)
        sin6 = sb("sin6", [128, NT, 6, 8], F32)
        ccos = sb("ccos", [128, 8], F32)
        csin = sb("csin", [128, 8], F32)
        impA = sb("impA", [128, NT * 32], F32)
        impB = sb("impB", [128, NT * 32], F32)
        wf = sb("wf", [128, 8, 16], BF)
        bfb = sb("bfb", [128, 16], F32)
        Lf = sb("Lf", [128, NT, 16], F32)
        Lsum = sb("Lsum", [128, 16], F32)
        Cc = sb("Cc", [128, NT * 16], F32)
        R1 = sb("R1", [128, NT * 16], F32)
        HI = sb("HI", [128, NT * 16], BF)
        LO = sb("LO", [128, NT * 16], BF)
        LO2 = sb("LO2", [128, NT * 16], BF)
        QAUG = sb("QAUG", [128, NT, 16, 6], BF)
        KAUG = sb("KAUG", [128, NT, 16, 6], BF)
        ftmp = sb("ftmp", [128, 16], F32)
        IMP = sb("IMP", [128, NT * 32], F32)
        M8 = sb("M8", [128, NT, 8], F32)
        imtmp = [sb("imtmp%d" % i, [128, 128], F32) for i in range(2)]
        fint = [sb("fint%d" % i, [128, 4, 64], F32) for i in range(2)]
        rtmp = [sb("rtmp%d" % i, [128, 6, 8], F32) for i in range(4)]
        gtmp = sb("gtmp", [128, 12], F32)
        PE32 = sb("PE32", [32, 128], F32)
        PEB = sb("PEB", [32, 128], BF)
        peT = sb("peT", [128, 32], BF)
        W2 = sb("W2", [128, 2, 2, 64], BF)
        constkv = sb("constkv", [128, 2, 2], F32)
        HID = sb("HID", [128, 2, 128], BF)
        hpre = sb("hpre", [128, 128], F32)
        hexp = sb("hexp", [128, 128], F32)
        KCs = sb("KCs", [128, 64], BF)
        KCT = sb("KCT", [96, 128], BF)
        VCA = sb("VCA", [128, 97], BF)
        ctmp = [sb("ctmp%d" % i, [128, 8], F32) for i in range(4)]
        sgt = [sb("sgt%d" % i, [128, 256], F32) for i in range(2)]
        mhalf = sb("mhalf", [128, 1], F32)
        ss = [sb("ss%d" % i, [128, 1], F32) for i in range(2)]
        rstd = [sb("rstd%d" % i, [128, 1], F32) for i in range(2)]
        rinv = [sb("rinv%d" % i, [128, 4], F32) for i in range(2)]
        scl = [sb("scl%d" % i, [128, 4], F32) for i in range(2)]
        oT = [sb("oT%d" % i, [128, 8, 128], BF) for i in range(2)]
        pA = [ps("pA%d" % i, [128, 512], F32) for i in range(2)]
        pTf = [ps("pT%d" % i, [128, 512], F32) for i in range(2)]
        pS = [ps("pS%d" % i, [128, 512], F32) for i in range(2)]
        pO = [ps("pO%d" % i, [128, 512], F32) for i in range(2)]
        pT = [t[:].bitcast(BF) for t in pTf]
        SB = [("pS", 0), ("pS", 1), ("pT", 1)]
        sbank = {("pS", 0): pS[0], ("pS", 1): pS[1], ("pT", 1): pTf[1]}

        P = Prog(nc)
        import os
        _skip = set(os.environ.get("KSKIP", "").split(","))

        def op(eng, fn, tag=None, **kw):
            if tag is not None and tag in _skip:
                return None
            return P.op(eng, fn, **kw)
        cnt = {"xt": 0, "stg": 0, "pt": 0, "sb": 0, "po": 0, "rt": 0, "w": 0, "wl": 0, "sl": 0, "it": 0, "ft": 0}

        def cload(dst_ap, src_ap, eng="sync"):
            op(eng, lambda e: e.dma_start(out=dst_ap, in_=src_ap), writes=["const"], dma="const", grp="const")

        cload(ident[:], din["c_ident"])
        cload(tri[:], din["c_tri"])
        cload(ones[:], din["c_ones"])
        cload(cm[:], din["c_cm"])
        cload(am[:], din["c_am"])
        cload(cmpmask[:], din["c_cmpmask"])
        cload(cos6[:].rearrange("p a b c -> p (a b c)"), din["c_cos6"])
        cload(sin6[:].rearrange("p a b c -> p (a b c)"), din["c_sin6"])
        cload(ccos[:], din["c_ccos"])
        cload(csin[:], din["c_csin"])
        cload(impA[:], din["c_impA"])
        cload(impB[:], din["c_impB"])
        cload(bfb[:], din["fox_b_f"][0:1, :].broadcast_to([128, 16]))
        cload(VCA[:, 65:97], din["c_ovl"])
        cload(PE32[:, 0:64], din["nsa_pe_k"])
        cload(PE32[:, 64:128], din["nsa_pe_v"])
        op("vector", lambda e: e.memset(Vaug[:, :, :, 64:65], 1.0), writes=["Vones"])
        op("vector", lambda e: e.memset(VCA[:, 64:65], 1.0), writes=["Vones"])
        op("vector", lambda e: e.memset(QAUG[:, :, :, 3:6], 1.0), writes=["augones"])
        op("vector", lambda e: e.memset(KAUG[:, :, :, 0:3], 1.0), writes=["augones"])
        for i in range(2):
            op("gpsimd", lambda e, i=i: e.memset(wbuf[i][:], 0.0), writes=[("wbuf", i)])
        op("gpsimd", lambda e: e.memset(KCs[:], 0.0), writes=["KCs"])
        op("gpsimd", lambda e: e.memset(KCT[:], 0.0), writes=["KCT"])
        op("gpsimd", lambda e: e.memset(QT[:].rearrange("p a b -> p (a b)"), 0.0), writes=[("QT", r, T) for r in range(4) for T in range(NT)])
        op("gpsimd", lambda e: e.memset(KT[:].rearrange("p a b -> p (a b)"), 0.0), writes=[("KT", r, T) for r in range(4) for T in range(NT)])
        op("gpsimd", lambda e: e.memset(stageQ[:].rearrange("p a b c -> p (a b c)"), 0.0), writes=[("stageQ", T) for T in range(NT)])
        op("gpsimd", lambda e: e.memset(mhalf[:], -0.5), writes=["mhalf"])
        op("gpsimd", lambda e: e.memset(HID[:], 0.0), writes=["HID"])

        def w3(slot):
            return wbuf[slot][:].rearrange("p (a b) -> p a b", b=1024)

        def load_w_cols(slot, wsrc, segs):
            for (c0, n, d0) in segs:
                src = wsrc[:, c0:c0 + n].rearrange("(fc p) n -> p fc n", p=128)
                dst = w3(slot)[:, :, d0:d0 + n]
                op("gpsimd", lambda e, src=src, dst=dst: e.dma_start(out=dst, in_=src),
                   writes=[("wbuf", slot)], dma=("w", slot), grp=("wl", cnt["wl"]))
            cnt["wl"] += 1

        def rms_rstd(xtile, s, junk, junk_res=None, xres=None):
            op("vector", lambda e: e.scalar_tensor_tensor(out=junk, in0=xtile, scalar=1.0, in1=xtile, op0=ALU.mult, op1=ALU.mult,
                                                          accum_out=ss[s][:]),
               reads=(xres if xres is not None else [("xt", s)]), writes=[("ss", s), junk_res if junk_res is not None else ("ht", s)])
            op("vector", lambda e: e.tensor_scalar(out=rstd[s][:], in0=ss[s][:], scalar1=1.0 / D, scalar2=EPS,
                                                   op0=ALU.mult, op1=ALU.add),
               reads=[("ss", s)], writes=[("rstd", s)])
            op("gpsimd", lambda e: e.tensor_tensor(out=rstd[s][:], in0=rstd[s][:], in1=mhalf[:], op=ALU.pow),
               reads=[("rstd", s), "mhalf"], writes=[("rstd", s)])

        def silu_psum(zin, out_ap, reads_, writes_, n):
            i = cnt["sl"] % 2
            cnt["sl"] += 1
            t = sgt[i][:, 0:n]
            op("scalar", lambda e: e.activation(out=t, in_=zin, func=AF.Tanh, scale=0.5), reads=reads_, writes=[("sgt", i)])
            op("vector", lambda e: e.tensor_scalar(out=t, in0=t, scalar1=0.5, scalar2=0.5, op0=ALU.mult, op1=ALU.add), reads=[("sgt", i)], writes=[("sgt", i)])
            op("vector", lambda e: e.tensor_tensor(out=out_ap, in0=zin, in1=t, op=ALU.mult), reads=reads_ + [("sgt", i)], writes=writes_)

        def phase_norm(L, xsrc):
            op("sync", lambda e: e.dma_start(out=gbc[:], in_=din["norm_g"][L:L + 1, :].broadcast_to([128, D])),
               writes=["gbc"], dma="gbc")
            def xslot(T):
                s4 = T % 4
                return (acc[:, 4 * s4:4 * s4 + 4, :].rearrange("p a b -> p (a b)"), [("acc", 4 * s4 + i) for i in range(4)], s4)

            def nfront(T):
                s = T % 2
                xo, xres, s4 = xslot(T)
                op("sync", lambda e: e.dma_start(out=xo, in_=xsrc[T * 128:(T + 1) * 128, :]),
                   reads=[("xd", T)], writes=xres, dma=("xo", s4))
                rms_rstd(xo, s, oT[s][:].rearrange("p a b -> p (a b)"), junk_res=("oT", s), xres=xres)

            def nback(T):
                s = T % 2
                xo, xres, s4 = xslot(T)
                op("vector", lambda e: e.scalar_tensor_tensor(out=ht[s][:], in0=xo, scalar=rstd[s][:, 0:1],
                                                              in1=gbc[:], op0=ALU.mult, op1=ALU.mult),
                   reads=xres + [("rstd", s), "gbc"], writes=[("ht", s)])
                pi = T % 2
                for fc in range(8):
                    op("tensor", lambda e, fc=fc: e.transpose(
                        out=pT[pi][:, fc * 128:(fc + 1) * 128], in_=ht[s][:, fc * 128:(fc + 1) * 128], identity=ident[:]),
                       reads=[("ht", s), "const"], writes=[("pT", pi)])
                op("scalar", lambda e: e.copy(out=hT[:, :, T * 128:(T + 1) * 128],
                                              in_=pT[pi][:, :].rearrange("p (a b) -> p a b", b=128)),
                   reads=[("pT", pi)], writes=[("hT", T)])

            nfront(0)
            for T in range(NT - 1):
                nfront(T + 1)
                nback(T)
            nback(NT - 1)

        _once = set()

        def fox_preload():
            if "fox" in _once:
                return
            _once.add("fox")
            Wf = din["fox_w_in"]
            srcf = Wf[:, 3072:3088].rearrange("(fc p) n -> p fc n", p=128)
            op("gpsimd", lambda e: e.dma_start(out=wf[:], in_=srcf), writes=["wf"], dma="wf")
            load_w_cols(0, Wf, [(0, 256, 0), (1024, 256, 256), (2048, 256, 512), (3088, 256, 768)])

        def nsa_preload_w1():
            if "w1" in _once:
                return
            _once.add("w1")
            for kv, nm in enumerate(["nsa_w_ck1", "nsa_w_cv1"]):
                p0 = 64 * kv
                w1v = wbuf[1][p0:p0 + 64, :].rearrange("p (l n) -> p l n", n=256)
                srcw = din[nm].rearrange("(l d) n -> d l n", d=64)
                for half in range(2):
                    op("gpsimd", lambda e, w1v=w1v, srcw=srcw, half=half: e.dma_start(
                        out=w1v[:, half * 16:(half + 1) * 16, :], in_=srcw[:, half * 16:(half + 1) * 16, :]),
                       writes=[("wbuf", 1)], dma=("w", 1), grp="w1load")

        _pref = {}

        def nsa_preload_g0():
            if "nsag0" in _once:
                return
            _once.add("nsag0")
            segs0 = [(0, 256, 0), (1536, 64, 256), (2048, 64, 320), (1024, 64, 384), (1280, 64, 448), (1792, 64, 512),
                     (2304, 64, 576), (2560, 12, 640), (2608, 256, 656)]
            load_w_cols(0, din["nsa_w_in"], segs0)

        def prefetch_out(L, wsrc, slot=None):
            key = (L, id(wsrc))
            if key not in _pref:
                if slot is None:
                    slot = cnt["w"] % 2
                    cnt["w"] += 1
                load_w_cols(slot, wsrc, [(0, 512, 0), (512, 512, 512)])
                gsrc = din["norm_g"][1:2, :] if L == 0 else din["final_g"][0:1, :]
                op("sync", lambda e: e.dma_start(out=gbc[:], in_=gsrc.broadcast_to([128, D])), writes=["gbc"], dma="gbc")
                _pref[key] = slot
            return _pref[key]

        def phase_out(L, xsrc, wsrc, fuse_next_norm):
            slot = prefetch_out(L, wsrc)
            def front(T):
                s = T % 2
                s4 = T % 4
                xo = acc[:, 4 * s4:4 * s4 + 4, :].rearrange("p a b -> p (a b)")
                xres = [("acc", 4 * s4 + i) for i in range(4)]
                hh = zs[:, 4 * s4:4 * s4 + 4, :].rearrange("p a b -> p (a b)")
                hres = [("zs", 4 * s4 + i) for i in range(4)]
                op("sync", lambda e, s=s, T=T: e.dma_start(out=ht[s][:], in_=og_d[T * 128:(T + 1) * 128, :]),
                   reads=[("ogd", T)], writes=[("ht", s)], dma=("ot", s))
                op("sync", lambda e, T=T, xo=xo: e.dma_start(out=xo, in_=xsrc[T * 128:(T + 1) * 128, :]),
                   reads=[("xd", T)], writes=xres, dma=("xo", s4))

            def trog(T):
                s = T % 2
                pi = T % 2
                for fc in range(8):
                    op("tensor", lambda e, s=s, fc=fc, pi=pi: e.transpose(
                        out=pT[pi][:, fc * 128:(fc + 1) * 128], in_=ht[s][:, fc * 128:(fc + 1) * 128], identity=ident[:]),
                       reads=[("ht", s), "const"], writes=[("pT", pi)])
                op("scalar", lambda e, s=s, pi=pi: e.copy(out=oT[s][:], in_=pT[pi][:, :].rearrange("p (a b) -> p a b", b=128)),
                   reads=[("pT", pi)], writes=[("oT", s)])

            def mmo(T):
                s = T % 2
                for c in range(2):
                    for fc in range(8):
                        op("tensor", lambda e, s=s, c=c, fc=fc: e.matmul(
                            pA[c][:, :], lhsT=oT[s][:, fc, :], rhs=w3(slot)[:, fc, c * 512:(c + 1) * 512],
                            start=(fc == 0), stop=(fc == 7)),
                           reads=[("oT", s), ("wbuf", slot)], writes=[("pA", c)])

            def front_b(T):
                s4 = T % 4
                xo = acc[:, 4 * s4:4 * s4 + 4, :].rearrange("p a b -> p (a b)")
                xres = [("acc", 4 * s4 + i) for i in range(4)]
                for c in range(2):
                    op("vector", lambda e, c=c, xo=xo: e.tensor_tensor(
                        out=xo[:, c * 512:(c + 1) * 512], in0=pA[c][:, :], in1=xo[:, c * 512:(c + 1) * 512], op=ALU.add),
                       reads=[("pA", c)] + xres, writes=xres)

            def back(T):
                s = T % 2
                s4 = T % 4
                xo = acc[:, 4 * s4:4 * s4 + 4, :].rearrange("p a b -> p (a b)")
                xres = [("acc", 4 * s4 + i) for i in range(4)]
                hh = zs[:, 4 * s4:4 * s4 + 4, :].rearrange("p a b -> p (a b)")
                hres = [("zs", 4 * s4 + i) for i in range(4)]
                pi = T % 2
                if L == 0:
                    op("gpsimd", lambda e, T=T, xo=xo: e.dma_start(out=x1_d[T * 128:(T + 1) * 128, :], in_=xo),
                       reads=xres, writes=[("xd", T)], dma=("xst", s4), final=debug)
                sr = cnt["xt"] % 2
                cnt["xt"] += 1
                op("vector", lambda e, xo=xo, hh=hh, sr=sr: e.scalar_tensor_tensor(out=hh, in0=xo, scalar=1.0, in1=xo, op0=ALU.mult, op1=ALU.mult,
                                                                         accum_out=ss[sr][:]),
                   reads=xres, writes=[("ss", sr)] + hres)
                op("vector", lambda e, sr=sr: e.tensor_scalar(out=rstd[sr][:], in0=ss[sr][:], scalar1=1.0 / D, scalar2=EPS,
                                                             op0=ALU.mult, op1=ALU.add),
                   reads=[("ss", sr)], writes=[("rstd", sr)])
                op("gpsimd", lambda e, sr=sr: e.tensor_tensor(out=rstd[sr][:], in0=rstd[sr][:], in1=mhalf[:], op=ALU.pow),
                   reads=[("rstd", sr), "mhalf"], writes=[("rstd", sr)])
                if L == 0:
                    if fuse_next_norm:
                        op("vector", lambda e, xo=xo, hh=hh, sr=sr: e.scalar_tensor_tensor(out=hh, in0=xo, scalar=rstd[sr][:, 0:1],
                                                                                 in1=gbc[:], op0=ALU.mult, op1=ALU.mult),
                           reads=xres + [("rstd", sr), "gbc"], writes=hres)
                        for fc in range(8):
                            op("tensor", lambda e, hh=hh, fc=fc, pi=pi: e.transpose(
                                out=pT[pi][:, fc * 128:(fc + 1) * 128], in_=hh[:, fc * 128:(fc + 1) * 128], identity=ident[:]),
                               reads=hres + ["const"], writes=[("pT", pi)])
                        op("scalar", lambda e, T=T, pi=pi: e.copy(out=hT[:, :, T * 128:(T + 1) * 128],
                                                                 in_=pT[pi][:, :].rearrange("p (a b) -> p a b", b=128)),
                           reads=[("pT", pi)], writes=[("hT", T)])
                else:
                    op("vector", lambda e, xo=xo, sr=sr: e.scalar_tensor_tensor(out=xo, in0=xo, scalar=rstd[sr][:, 0:1],
                                                                      in1=gbc[:], op0=ALU.mult, op1=ALU.mult),
                       reads=xres + [("rstd", sr), "gbc"], writes=xres)
                    op("gpsimd", lambda e, T=T, xo=xo: e.dma_start(out=out_d[T * 128:(T + 1) * 128, :], in_=xo),
                       reads=xres, writes=[("outd", T)], dma=("xst", s4), final=True)


            front(0)
            front(1)
            trog(0)
            for T in range(NT):
                if T + 2 < NT:
                    front(T + 2)
                if T + 1 < NT:
                    trog(T + 1)
                mmo(T)
                if T >= 1:
                    back(T - 1)
                front_b(T)
            back(NT - 1)

        class Unit:
            pass

        LA = 2
        FILL = int(os.environ.get("KFILL", "0"))

        def emit_qk(ch):
            si = cnt["sb"] % 3
            cnt["sb"] += 1
            key = SB[si]
            bank = sbank[key]
            npart = ch["np"]
            for (coff, ncols, lhsT, rhs, mask, rd) in ch["qk"]:
                op("tensor", lambda e, coff=coff, ncols=ncols, lhsT=lhsT, rhs=rhs, mask=mask:
                   e.matmul(bank[0:npart, coff:coff + ncols], lhsT=lhsT, rhs=rhs, start=True, stop=(mask is None)),
                   reads=rd, writes=[key])
                if mask is not None:
                    op("tensor", lambda e, coff=coff, ncols=ncols, mask=mask:
                       e.matmul(bank[0:npart, coff:coff + ncols], lhsT=mask[0], rhs=mask[1], start=False, stop=True),
                       reads=["const"], writes=[key])
            if FILL > 0:
                op("tensor", lambda e: e.matmul(pTf[1][:, 0:FILL], lhsT=ident[:], rhs=cmpmask[:, 0:FILL], start=True, stop=True),
                   reads=["const"], writes=[("pT", 1)])
            pi = cnt["pt"] % 4
            cnt["pt"] += 1
            ch["pt"] = pi
            w = ch["w"]
            op("scalar", lambda e, pi=pi, w=w: e.activation(out=PT[pi][0:npart, 0:w], in_=bank[0:npart, 0:w], func=AF.Exp),
               reads=[key], writes=[("PT", pi)])

        def emit_pv(ch):
            pi = ch["pt"]
            npart = ch["np"]
            u = ch["unit"]
            for (j, rhs, ooff, on, st, sp, rd) in ch["pv"]:
                op("tensor", lambda e, j=j, rhs=rhs, ooff=ooff, on=on, st=st, sp=sp, pi=pi:
                   e.matmul(pO[u.po][:, ooff:ooff + on], lhsT=PT[pi][0:npart, j * 128:(j + 1) * 128], rhs=rhs,
                            start=st, stop=sp),
                   reads=[("PT", pi)] + rd, writes=[("pO", u.po)])
            if ch.get("last"):
                u.fin(u)

        class Streamer:
            def __init__(self):
                self.pending = []

            def push(self, ch):
                emit_qk(ch)
                self.pending.append(ch)
                if len(self.pending) > LA:
                    emit_pv(self.pending.pop(0))

            def flush(self):
                while self.pending:
                    emit_pv(self.pending.pop(0))

            def run(self, chunks, bg=(), free=()):
                items = list(free) + list(bg)
                nfree = len(free)
                done = 0
                n = len(chunks)
                for i, ch in enumerate(chunks):
                    self.push(ch)
                    tgt = (len(items) * (i + 1)) // max(1, n)
                    if i + 1 <= LA:
                        tgt = min(tgt, nfree)
                    while done < tgt:
                        items[done]()
                        done += 1
                while done < len(items):
                    items[done]()
                    done += 1

        def attn_stream(chunks):
            st_ = Streamer()
            st_.run(chunks)
            st_.flush()

        def new_unit(fin, po=None, ooff=0):
            u = Unit()
            if po is None:
                po = cnt["po"] % 2
                cnt["po"] += 1
            u.po = po
            u.ooff = ooff
            u.fin = fin
            return u

        def next_po():
            po = cnt["po"] % 2
            cnt["po"] += 1
            return po

        def nofin(u):
            pass

        def blocks_chunks(u, qTap, qres, kT_fn, kres_fn, v_fn, vres_fn, kbs, qb, masks):
            chs = []
            for c0 in range(0, len(kbs), 4):
                sub = kbs[c0:c0 + 4]
                ch = {"np": 128, "w": len(sub) * 128, "unit": u, "qk": [], "pv": []}
                for j, kb in enumerate(sub):
                    ch["qk"].append((j * 128, 128, kT_fn(kb), qTap, masks.get(kb), [kres_fn(kb), qres, "const"]))
                    ch["pv"].append((j, v_fn(kb), u.ooff, 65, kb == kbs[0], kb == kbs[-1], [vres_fn(kb), "Vones"]))
                chs.append(ch)
            chs[-1]["last"] = True
            return chs

        def fox_layer(out_pref=None):
            W = din["fox_w_in"]
            fox_preload()
            for T in range(NT):
                a = T % 2
                for fc in range(8):
                    op("tensor", lambda e, T=T, fc=fc, a=a: e.matmul(pA[a][:, 0:16], lhsT=hT[:, fc, T * 128:(T + 1) * 128],
                                                                    rhs=wf[:, fc, :], start=(fc == 0), stop=(fc == 7)),
                       reads=[("hT", T), "wf"], writes=[("pA", a)])
                op("vector", lambda e, a=a: e.tensor_tensor(out=ftmp[:], in0=pA[a][:, 0:16], in1=bfb[:], op=ALU.add),
                   reads=[("pA", a), "const"], writes=["ftmp"])
                op("scalar", lambda e: e.activation(out=ftmp[:], in_=ftmp[:], func=AF.Exp, scale=-1.0),
                   reads=["ftmp"], writes=["ftmp"])
                op("scalar", lambda e, T=T: e.activation(out=Lf[:, T, :], in_=ftmp[:], func=AF.Ln, bias=1.0),
                   reads=["ftmp"], writes=[("Lf", T)])
            for T in range(NT):
                a = T % 2
                op("tensor", lambda e, T=T, a=a: e.matmul(pA[a][:, 0:16], lhsT=tri[:], rhs=Lf[:, T, :], start=True, stop=(T == 0)),
                   reads=[("Lf", T), "const"], writes=[("pA", a)])
                if T > 0:
                    if T == 1:
                        op("vector", lambda e: e.tensor_copy(out=Lsum[:], in_=Lf[:, 0, :]), reads=[("Lf", 0)], writes=["Lsum"])
                    else:
                        op("vector", lambda e, T=T: e.tensor_tensor(out=Lsum[:], in0=Lsum[:], in1=Lf[:, T - 1, :], op=ALU.add),
                           reads=["Lsum", ("Lf", T - 1)], writes=["Lsum"])
                    op("tensor", lambda e, a=a: e.matmul(pA[a][:, 0:16], lhsT=ones[:], rhs=Lsum[:], start=False, stop=True),
                       reads=["Lsum", "const"], writes=[("pA", a)])
                op("vector", lambda e, T=T, a=a: e.tensor_scalar(out=Cc[:, T * 16:(T + 1) * 16], in0=pA[a][:, 0:16], scalar1=-1.0,
                                                                scalar2=None, op0=ALU.mult),
                   reads=[("pA", a)], writes=["Cc"])
            v = "vector"
            op(v, lambda e: e.tensor_copy(out=HI[:], in_=Cc[:]), reads=["Cc"], writes=["HI"])
            op(v, lambda e: e.tensor_tensor(out=R1[:], in0=Cc[:], in1=HI[:], op=ALU.subtract), reads=["Cc", "HI"], writes=["R1"])
            op(v, lambda e: e.tensor_copy(out=LO[:], in_=R1[:]), reads=["R1"], writes=["LO"])
            op(v, lambda e: e.tensor_tensor(out=R1[:], in0=R1[:], in1=LO[:], op=ALU.subtract), reads=["R1", "LO"], writes=["R1"])
            op(v, lambda e: e.tensor_copy(out=LO2[:], in_=R1[:]), reads=["R1"], writes=["LO2"])
            for i, src_t in enumerate([HI, LO, LO2]):
                sv = src_t[:].rearrange("p (a b) -> p a b", b=16)
                op(v, lambda e, i=i, sv=sv: e.tensor_copy(out=QAUG[:, :, :, i], in_=sv), reads=["HI", "LO", "LO2", "augones"], writes=["QAUG"])
                op(v, lambda e, i=i, sv=sv: e.tensor_scalar(out=KAUG[:, :, :, 3 + i], in0=sv, scalar1=-1.0, scalar2=None, op0=ALU.mult),
                   reads=["HI", "LO", "LO2", "augones"], writes=["KAUG"])

            chk("decay")
            def fox_stages(u_, slot, alt=False):

                def bk(T):
                    if alt and T % 2 == 1:
                        return ((("pS", 0), pS[0]), (("pS", 1), pS[1]))
                    return ((("pA", 0), pA[0]), (("pA", 1), pA[1]))
                def S1(T):
                    def mk(c, h):
                        def f():
                            for fc in range(h, h + 1):
                                op("tensor", lambda e, fc=fc: e.matmul(
                                    bk(T)[c][1][:, :], lhsT=hT[:, fc, T * 128:(T + 1) * 128], rhs=w3(slot)[:, fc, c * 512:(c + 1) * 512],
                                    start=(fc == 0), stop=(fc == 7)),
                                   reads=[("hT", T), ("wbuf", slot)], writes=[bk(T)[c][0]])
                        return f
                    return [mk(c, h) for c in range(2) for h in range(8)]

                def S2(T):
                    s = T % 2
                    i = T % 2
                    t = sgt[i][:, 0:256]

                    def a():
                        op("scalar", lambda e: e.activation(out=t, in_=bk(T)[1][1][:, 256:512], func=AF.Tanh, scale=0.5), reads=[bk(T)[1][0]], writes=[("sgt", i)])

                    def b():
                        op("vector", lambda e: e.tensor_scalar(out=stageQ[:, T, :, 0:64], in0=bk(T)[0][1][:, 0:256].rearrange("p (r d) -> p r d", d=64),
                                                               scalar1=0.125, scalar2=None, op0=ALU.mult),
                           reads=[bk(T)[0][0]], writes=[("stageQ", T)])
                        op("vector", lambda e: e.tensor_copy(out=stg[s][:, :, 0:64], in_=bk(T)[0][1][:, 256:512].rearrange("p (r d) -> p r d", d=64)),
                           reads=[bk(T)[0][0]], writes=[("stg", s)])

                    def c():
                        op("gpsimd", lambda e: e.tensor_copy(out=stageQ[:, T, :, 64:70], in_=QAUG[:, T, 4 * u_:4 * u_ + 4, :]),
                           reads=["QAUG"], writes=[("stageQ", T)])
                        op("gpsimd", lambda e: e.tensor_copy(out=stg[s][:, :, 64:70], in_=KAUG[:, T, 4 * u_:4 * u_ + 4, :]),
                           reads=["KAUG"], writes=[("stg", s)])

                    def d():
                        op("vector", lambda e: e.tensor_copy(out=Vaug[:, T, :, 0:64], in_=bk(T)[1][1][:, 0:256].rearrange("p (r d) -> p r d", d=64)),
                           reads=[bk(T)[1][0]], writes=[("V", T)])
                        op("vector", lambda e: e.tensor_scalar(out=t, in0=t, scalar1=0.5, scalar2=0.5, op0=ALU.mult, op1=ALU.add), reads=[("sgt", i)], writes=[("sgt", i)])
                        op("vector", lambda e: e.tensor_tensor(out=zs[:, T, :], in0=bk(T)[1][1][:, 256:512], in1=t, op=ALU.mult),
                           reads=[bk(T)[1][0], ("sgt", i)], writes=[("zs", T)])
                    return [a, b, c, d]

                def S3(T):
                    s = T % 2

                    def tr():
                        for r in range(4):
                            op("tensor", lambda e, r=r: e.transpose(out=pT[0][0:70, r * 128:(r + 1) * 128], in_=stageQ[:, T, r, 0:70], identity=ident[:]),
                               reads=[("stageQ", T), "const"], writes=[("pT", 0)])

                    def tr2():
                        for r in range(4):
                            op("tensor", lambda e, r=r: e.transpose(out=pT[0][0:70, 512 + r * 128:512 + (r + 1) * 128], in_=stg[s][:, r, 0:70], identity=ident[:]),
                               reads=[("stg", s), "const"], writes=[("pT", 0)])

                    def ev():
                        op("vector", lambda e: e.tensor_copy(out=QT[0:70, :, T * 128:(T + 1) * 128],
                                                             in_=pT[0][0:70, 0:512].rearrange("p (r t) -> p r t", t=128)),
                           reads=[("pT", 0)], writes=[("QT", r, T) for r in range(4)])

                    def ev2():
                        op("vector", lambda e: e.tensor_copy(out=KT[0:70, :, T * 128:(T + 1) * 128],
                                                             in_=pT[0][0:70, 512:1024].rearrange("p (r t) -> p r t", t=128)),
                           reads=[("pT", 0)], writes=[("KT", r, T) for r in range(4)])
                    return [tr, tr2, ev, ev2]
                return S1, S2, S3

            def run_all(fs):
                for f in fs:
                    f()

            chk("decay")
            slots = [0, 1, 0, 1]
            wsegs = lambda u_: [(256 * u_, 256, 0), (1024 + 256 * u_, 256, 256), (2048 + 256 * u_, 256, 512), (3088 + 256 * u_, 256, 768)]
            fox_preload()
            S1, S2, S3 = fox_stages(0, slots[0], alt=True)
            run_all(S1(0))
            for T in range(NT):
                if T + 1 < NT:
                    run_all(S1(T + 1))
                run_all(S2(T))
                if T >= 1:
                    run_all(S3(T - 1))
            run_all(S3(NT - 1))
            for u_ in range(4):
                nxt = None
                if u_ + 1 < 4:
                    load_w_cols(slots[u_ + 1], W, wsegs(u_ + 1))
                    nxt = fox_stages(u_ + 1, slots[u_ + 1])
                elif out_pref is not None:
                    out_pref()
                tdone = {}

                def tile_done(T, u_=u_):
                    dst = og_d[T * 128:(T + 1) * 128, 256 * u_:256 * u_ + 256]
                    op("sync", lambda e: e.dma_start(out=dst, in_=zs[:, T, :]), reads=[("zs", T)], writes=[("ogd", T)], dma="ogst")

                strm = Streamer()
                for T in range(NT - 1, -1, -1):
                    chunks = []
                    qb = T
                    po = next_po()

                    def fin(u, qb=qb, po=po):
                        ri = cnt["rt"] % 2
                        cnt["rt"] += 1
                        fi = cnt["ft"] % 2
                        cnt["ft"] += 1
                        po4 = pO[po][:, 0:260].rearrange("p (r c) -> p r c", c=65)
                        op("vector", lambda e: e.reciprocal(out=rinv[ri][:].unsqueeze(2), in_=po4[:, :, 64:65]),
                           reads=[("pO", po)], writes=[("rinv", ri)])
                        op("vector", lambda e: e.tensor_tensor(out=fint[fi][:], in0=po4[:, :, 0:64],
                                                               in1=rinv[ri][:].unsqueeze(2).broadcast_to([128, 4, 64]), op=ALU.mult),
                           reads=[("pO", po), ("rinv", ri)], writes=[("fint", fi)])
                        zv = zs[:, qb, :].rearrange("p (r d) -> p r d", d=64)
                        op("gpsimd", lambda e: e.tensor_tensor(out=zv, in0=zv, in1=fint[fi][:], op=ALU.mult),
                           reads=[("fint", fi), ("zs", qb)], writes=[("zs", qb)])
                        tdone[qb] = 4
                        tile_done(qb)
                    for r in range(4):
                        u = new_unit(fin if r == 3 else nofin, po=po, ooff=65 * r)
                        qTap = QT[0:128, r, qb * 128:(qb + 1) * 128]
                        chunks += blocks_chunks(
                            u, qTap, ("QT", r, qb),
                            lambda kb, r=r: KT[0:128, r, kb * 128:(kb + 1) * 128], lambda kb, r=r: ("KT", r, kb),
                            lambda kb, r=r: Vaug[:, kb, r, 0:65], lambda kb: ("V", kb),
                            list(range(qb + 1)), qb, {qb: (ident[:], cm[:])})
                    bg = []
                    if nxt is not None:
                        n1, n2, n3 = nxt
                        free = []
                        if T + 1 < NT:
                            free = n1(T + 1)
                            bg.append(lambda T=T: tdone.get(T + 1) == 4 or _raise("tile not released"))
                            bg += n2(T + 1)
                        if T + 2 < NT:
                            bg += n3(T + 2)
                        strm.run(chunks, bg, free)
                    else:
                        strm.run(chunks, bg)
                strm.flush()
                if nxt is not None:
                    n1, n2, n3 = nxt
                    run_all(n3(1) + n1(0) + n2(0) + n3(0))
            chk("fox")

        def nsa_layer(out_pref=None):
            def run_all(fs):
                for f in fs:
                    f()

            W = din["nsa_w_in"]
            op("vector", lambda e: e.tensor_copy(out=PEB[:], in_=PE32[:]), reads=["const"], writes=["PEB"])
            op("tensor", lambda e: e.transpose(out=pT[0][:, 0:32], in_=PEB[:], identity=ident[0:32, 0:32]),
               reads=["PEB", "const"], writes=[("pT", 0)])
            op("vector", lambda e: e.tensor_copy(out=peT[:], in_=pT[0][:, 0:32]), reads=[("pT", 0)], writes=["peT"])
            for kv, nm in enumerate(["nsa_w_ck2", "nsa_w_cv2"]):
                src = din[nm].rearrange("(c p) d -> p c d", p=128)
                op("gpsimd", lambda e, kv=kv, src=src: e.dma_start(out=W2[:, kv, :, :], in_=src), writes=["W2"], dma="w2")
            op("sync", lambda e: e.dma_start(out=KT[64:96, 0, :], in_=din["c_eall"]),
               writes=[("KT", 0, T) for T in range(NT)], dma="eall")
            op("gpsimd", lambda e: e.memset(KT[64:96, 1, :], 0.0), writes=[("KT", 1, T) for T in range(NT)])

            def nsa_segs(g):
                return [(256 * g, 256, 0), (1536 + 64 * g, 64, 256), (2048 + 64 * g, 64, 320), (1024 + 64 * g, 64, 384),
                        (1280 + 64 * g, 64, 448), (1792 + 64 * g, 64, 512), (2304 + 64 * g, 64, 576),
                        (2560 + 12 * g, 12, 640), (2608 + 256 * g, 256, 656)]

            WS = 0
            W1S = 1

            def nsa_stages(g, alt=False):

                def bk(T):
                    if alt and T % 2 == 1:
                        return ((("pS", 0), pS[0]), (("pS", 1), pS[1]))
                    return ((("pA", 0), pA[0]), (("pA", 1), pA[1]))
                def S1(T):
                    def mk(c, c0, nn, h):
                        def f():
                            for fc in range(h, h + 1):
                                op("tensor", lambda e, fc=fc: e.matmul(
                                    bk(T)[c][1][:, 0:nn], lhsT=hT[:, fc, T * 128:(T + 1) * 128], rhs=w3(WS)[:, fc, c0:c0 + nn],
                                    start=(fc == 0), stop=(fc == 7)),
                                   reads=[("hT", T), ("wbuf", WS)], writes=[bk(T)[c][0]])
                        return f
                    return [mk(c, c0, nn, h) for (c, c0, nn) in ((0, 0, 512), (1, 512, 400)) for h in range(8)]

                def S2(T):
                    k0, k1 = bk(T)[0][0], bk(T)[1][0]
                    b0, b1 = bk(T)[0][1], bk(T)[1][1]
                    s = T % 2
                    sg = stg[s][:].rearrange("p a b -> p (a b)")
                    p6 = b0[:, 0:384].rearrange("p (s d) -> p s d", d=64)
                    x1 = p6[:, :, 0:8]
                    x2 = p6[:, :, 8:16]
                    i = T % 2
                    t = sgt[i][:, 0:256]
                    sg3 = sg[:, 0:128].rearrange("p (s d) -> p s d", d=64)

                    def a():
                        op("vector", lambda e: e.tensor_tensor(out=rtmp[0][:], in0=x1, in1=cos6[:, T, :, :], op=ALU.mult),
                           reads=[k0, "const"], writes=[("rtmp", 0)])
                        op("vector", lambda e: e.tensor_tensor(out=rtmp[1][:], in0=x2, in1=sin6[:, T, :, :], op=ALU.mult),
                           reads=[k0, "const"], writes=[("rtmp", 1)])
                        op("vector", lambda e: e.tensor_tensor(out=rtmp[2][:], in0=x1, in1=sin6[:, T, :, :], op=ALU.mult),
                           reads=[k0, "const"], writes=[("rtmp", 2)])
                        op("vector", lambda e: e.tensor_tensor(out=rtmp[3][:], in0=x2, in1=cos6[:, T, :, :], op=ALU.mult),
                           reads=[k0, "const"], writes=[("rtmp", 3)])

                    def b():
                        op("vector", lambda e: e.tensor_scalar(out=stageQ[:, T, :, 16:64], in0=p6[:, 0:4, 16:64], scalar1=0.125, scalar2=None, op0=ALU.mult),
                           reads=[k0], writes=[("stageQ", T)])
                        op("vector", lambda e: e.tensor_copy(out=sg3[:, :, 16:64], in_=p6[:, 4:6, 16:64]),
                           reads=[k0], writes=[("stg", s)])
                        op("vector", lambda e: e.tensor_copy(out=sg[:, 128:256], in_=b0[:, 384:512]),
                           reads=[k0], writes=[("stg", s)])

                    def c():
                        op("gpsimd", lambda e: e.tensor_tensor(out=stageQ[:, T, :, 0:8], in0=rtmp[0][:, 0:4, :], in1=rtmp[1][:, 0:4, :], op=ALU.subtract),
                           reads=[("rtmp", 0), ("rtmp", 1)], writes=[("stageQ", T)])
                        op("gpsimd", lambda e: e.tensor_tensor(out=stageQ[:, T, :, 8:16], in0=rtmp[2][:, 0:4, :], in1=rtmp[3][:, 0:4, :], op=ALU.add),
                           reads=[("rtmp", 2), ("rtmp", 3)], writes=[("stageQ", T)])
                        op("gpsimd", lambda e: e.tensor_tensor(out=sg3[:, :, 0:8], in0=rtmp[0][:, 4:6, :], in1=rtmp[1][:, 4:6, :], op=ALU.subtract),
                           reads=[("rtmp", 0), ("rtmp", 1)], writes=[("stg", s)])
                        op("gpsimd", lambda e: e.tensor_tensor(out=sg3[:, :, 8:16], in0=rtmp[2][:, 4:6, :], in1=rtmp[3][:, 4:6, :], op=ALU.add),
                           reads=[("rtmp", 2), ("rtmp", 3)], writes=[("stg", s)])

                    def d():
                        op("scalar", lambda e: e.activation(out=t, in_=b1[:, 144:400], func=AF.Tanh, scale=0.5), reads=[k1], writes=[("sgt", i)])
                        op("scalar", lambda e: e.activation(out=gtmp[:], in_=b1[:, 128:140], func=AF.Exp, scale=-1.0),
                           reads=[k1], writes=["gtmp"])

                    def e_():
                        op("vector", lambda e: e.tensor_copy(out=Vaug[:, T, 0:2, 0:64], in_=b1[:, 0:128].rearrange("p (s d) -> p s d", d=64)),
                           reads=[k1], writes=[("V", T)])
                        op("vector", lambda e: e.tensor_scalar(out=t, in0=t, scalar1=0.5, scalar2=0.5, op0=ALU.mult, op1=ALU.add), reads=[("sgt", i)], writes=[("sgt", i)])
                        op("vector", lambda e: e.tensor_tensor(out=zs[:, T, :], in0=b1[:, 144:400], in1=t, op=ALU.mult),
                           reads=[k1, ("sgt", i)], writes=[("zs", T)])
                        op("vector", lambda e: e.tensor_scalar(out=gtmp[:], in0=gtmp[:], scalar1=1.0, scalar2=None, op0=ALU.add),
                           reads=["gtmp"], writes=["gtmp"])
                        op("vector", lambda e: e.reciprocal(out=G[:, T, :], in_=gtmp[:]), reads=["gtmp"], writes=[("G", T)])
                    return [a, d, b, c, e_]

                def S3(T):
                    s = T % 2
                    sg = stg[s][:].rearrange("p a b -> p (a b)")

                    def tr():
                        for r in range(4):
                            op("tensor", lambda e, r=r: e.transpose(out=pT[0][0:96, r * 128:(r + 1) * 128], in_=stageQ[:, T, r, 0:96], identity=ident[:]),
                               reads=[("stageQ", T), "const"], writes=[("pT", 0)])
                        op("tensor", lambda e: e.transpose(out=pT[0][:, 512:640], in_=sg[:, 0:128], identity=ident[:]),
                           reads=[("stg", s), "const"], writes=[("pT", 0)])
                        op("tensor", lambda e: e.transpose(out=pT[0][:, 640:768], in_=sg[:, 128:256], identity=ident[:]),
                           reads=[("stg", s), "const"], writes=[("pT", 0)])

                    def ev():
                        op("vector", lambda e: e.tensor_copy(out=QT[0:64, :, T * 128:(T + 1) * 128],
                                                             in_=pT[0][0:64, 0:512].rearrange("p (r t) -> p r t", t=128)),
                           reads=[("pT", 0)], writes=[("QT", r, T) for r in range(4)])
                        op("vector", lambda e: e.tensor_copy(out=KT[0:64, 0, T * 128:(T + 1) * 128], in_=pT[0][0:64, 512:640]),
                           reads=[("pT", 0)], writes=[("KT", 0, T)])

                    def ev2():
                        op("vector", lambda e: e.tensor_copy(out=KT[0:64, 1, T * 128:(T + 1) * 128], in_=pT[0][64:128, 512:640]),
                           reads=[("pT", 0)], writes=[("KT", 1, T)])
                        op("vector", lambda e: e.tensor_copy(out=KT[:, 2, :].rearrange("p (b c) -> p b c", c=128)[:, :, 8 * T:8 * T + 8],
                                                             in_=pT[0][:, 640:768].rearrange("p (a b) -> p b a", b=16)),
                           reads=[("pT", 0)], writes=[("KT", 2, T)])
                    return [tr, ev, ev2]
                return S1, S2, S3

            w1view = {}
            for kv, nm in enumerate(["nsa_w_ck1", "nsa_w_cv1"]):
                p0 = 64 * kv
                w1v = wbuf[W1S][p0:p0 + 64, :].rearrange("p (l n) -> p l n", n=256)
                w1view[kv] = w1v
            nsa_preload_w1()
            for kv in range(2):
                p0 = 64 * kv
                for nch in range(2):
                    for l in range(32):
                        op("tensor", lambda e, nch=nch, l=l, kv=kv, p0=p0: e.matmul(
                            pA[0][:, 2 * kv + nch:2 * kv + nch + 1], lhsT=w1view[kv][:, l, nch * 128:(nch + 1) * 128],
                            rhs=peT[p0:p0 + 64, l:l + 1], start=(l == 0), stop=(l == 31)),
                           reads=[("wbuf", W1S), "peT"], writes=[("pA", 0)])
                op("vector", lambda e, kv=kv: e.tensor_copy(out=constkv[:, kv, :], in_=pA[0][:, 2 * kv:2 * kv + 2]),
                   reads=[("pA", 0)], writes=["constkv"])

            nsa_preload_g0()
            S1, S2, S3 = nsa_stages(0, alt=True)
            run_all(S1(0))
            for T in range(NT):
                if T + 1 < NT:
                    run_all(S1(T + 1))
                run_all(S2(T))
                if T >= 1:
                    run_all(S3(T - 1))
            run_all(S3(NT - 1))

            for g in range(4):
                nxt = None
                if g + 1 < 4:
                    load_w_cols(WS, W, nsa_segs(g + 1))
                    nxt = nsa_stages(g + 1)
                kraw_res = [("KT", 2, T) for T in range(NT)]
                for kv in range(2):
                    p0 = 64 * kv
                    w1v = w1view[kv]
                    raw = KT[p0:p0 + 64, 2, :].rearrange("p (b c) -> p b c", c=128)
                    for nch in range(2):
                        for l in range(32):
                            rhs = raw[:, l, 0:127] if l < 16 else raw[:, l - 16, 1:128]
                            op("tensor", lambda e, nch=nch, l=l, w1v=w1v, rhs=rhs: e.matmul(
                                pA[nch][:, 0:127], lhsT=w1v[:, l, nch * 128:(nch + 1) * 128], rhs=rhs,
                                start=(l == 0), stop=(l == 31)),
                               reads=[("wbuf", W1S)] + kraw_res, writes=[("pA", nch)])
                        op("scalar", lambda e, nch=nch, kv=kv: e.activation(out=hpre[:, 0:127], in_=pA[nch][:, 0:127], func=AF.Identity,
                                                                            bias=constkv[:, kv, nch:nch + 1]),
                           reads=[("pA", nch), "constkv"], writes=["hpre"])
                        op("scalar", lambda e: e.activation(out=hexp[:, 0:127], in_=hpre[:, 0:127], func=AF.Tanh, scale=0.5),
                           reads=["hpre"], writes=["hexp"])
                        op("vector", lambda e: e.tensor_scalar(out=hexp[:, 0:127], in0=hexp[:, 0:127], scalar1=0.5, scalar2=0.5, op0=ALU.mult, op1=ALU.add),
                           reads=["hexp"], writes=["hexp"])
                        op("vector", lambda e, nch=nch: e.tensor_tensor(out=HID[:, nch, 0:127], in0=hpre[:, 0:127], in1=hexp[:, 0:127], op=ALU.mult),
                           reads=["hpre", "hexp"], writes=["HID"])
                    for nch in range(2):
                        op("tensor", lambda e, nch=nch, kv=kv: e.matmul(pA[0][0:127, 0:64], lhsT=HID[:, nch, 0:127], rhs=W2[:, kv, nch, :],
                                                                      start=(nch == 0), stop=(nch == 1)),
                           reads=["HID", "W2"], writes=[("pA", 0)])
                    if kv == 0:
                        kx1 = pA[0][0:127, 0:8]
                        kx2 = pA[0][0:127, 8:16]
                        rd = [("pA", 0), "const"]
                        op("vector", lambda e: e.tensor_tensor(out=ctmp[0][0:127, :], in0=kx1, in1=ccos[0:127, :], op=ALU.mult), reads=rd, writes=[("ctmp", 0)])
                        op("vector", lambda e: e.tensor_tensor(out=ctmp[1][0:127, :], in0=kx2, in1=csin[0:127, :], op=ALU.mult), reads=rd, writes=[("ctmp", 1)])
                        op("vector", lambda e: e.tensor_tensor(out=ctmp[2][0:127, :], in0=kx1, in1=csin[0:127, :], op=ALU.mult), reads=rd, writes=[("ctmp", 2)])
                        op("vector", lambda e: e.tensor_tensor(out=ctmp[3][0:127, :], in0=kx2, in1=ccos[0:127, :], op=ALU.mult), reads=rd, writes=[("ctmp", 3)])
                        op("vector", lambda e: e.tensor_copy(out=KCs[0:127, 16:64], in_=pA[0][0:127, 16:64]), reads=[("pA", 0)], writes=["KCs"])
                        op("vector", lambda e: e.tensor_tensor(out=KCs[0:127, 0:8], in0=ctmp[0][0:127, :], in1=ctmp[1][0:127, :], op=ALU.subtract),
                           reads=[("ctmp", 0), ("ctmp", 1)], writes=["KCs"])
                        op("vector", lambda e: e.tensor_tensor(out=KCs[0:127, 8:16], in0=ctmp[2][0:127, :], in1=ctmp[3][0:127, :], op=ALU.add),
                           reads=[("ctmp", 2), ("ctmp", 3)], writes=["KCs"])
                        op("tensor", lambda e: e.transpose(out=pT[0][0:64, 0:127], in_=KCs[0:127, 0:64], identity=ident[0:127, 0:127]),
                           reads=["KCs", "const"], writes=[("pT", 0)])
                        op("vector", lambda e: e.tensor_copy(out=KCT[0:64, 0:127], in_=pT[0][0:64, 0:127]), reads=[("pT", 0)], writes=["KCT"])
                    else:
                        op("vector", lambda e: e.tensor_copy(out=VCA[0:127, 0:64], in_=pA[0][0:127, 0:64]), reads=[("pA", 0)], writes=["VCA"])

                if nxt is None and out_pref is not None:
                    out_pref()
                def cmp_chunk(r, qc, po):
                    def fin(u, r=r, qc=qc):
                        ri = cnt["rt"] % 2
                        cnt["rt"] += 1
                        po4 = pO[u.po][:, 0:388].rearrange("p (j c) -> p j c", c=97)
                        op("vector", lambda e: e.tensor_scalar(out=rinv[ri][:].unsqueeze(2), in0=po4[:, :, 64:65], scalar1=1e-30, scalar2=None, op0=ALU.add),
                           reads=[("pO", u.po)], writes=[("rinv", ri)])
                        op("vector", lambda e: e.reciprocal(out=rinv[ri][:], in_=rinv[ri][:]), reads=[("rinv", ri)], writes=[("rinv", ri)])
                        op("vector", lambda e: e.tensor_tensor(out=scl[ri][:], in0=rinv[ri][:], in1=G[:, 4 * qc:4 * qc + 4, 3 * r], op=ALU.mult),
                           reads=[("rinv", ri)] + [("G", 4 * qc + j) for j in range(4)], writes=[("scl", ri)])
                        accres = [("acc", 4 * qc + j) for j in range(4)]
                        impres = [("IMP", 4 * qc + j) for j in range(4)]
                        op("vector", lambda e: e.tensor_tensor(
                            out=acc[:, 4 * qc:4 * qc + 4, r * 64:(r + 1) * 64], in0=po4[:, :, 0:64],
                            in1=scl[ri][:].unsqueeze(2).broadcast_to([128, 4, 64]), op=ALU.mult),
                           reads=[("pO", u.po), ("scl", ri)], writes=accres)
                        IMPv = IMP[:, 4 * qc * 32:(4 * qc + 4) * 32].rearrange("p (j c) -> p j c", c=32)
                        rbc = rinv[ri][:].unsqueeze(2).broadcast_to([128, 4, 32])
                        if r == 0:
                            op("vector", lambda e: e.tensor_tensor(out=IMPv, in0=po4[:, :, 65:97], in1=rbc, op=ALU.mult),
                               reads=[("pO", u.po), ("rinv", ri)], writes=impres)
                        else:
                            it = cnt["it"] % 2
                            cnt["it"] += 1
                            tv = imtmp[it][:].rearrange("p (j c) -> p j c", c=32)
                            op("vector", lambda e: e.tensor_tensor(out=tv, in0=po4[:, :, 65:97], in1=rbc, op=ALU.mult),
                               reads=[("pO", u.po), ("rinv", ri)], writes=[("imtmp", it)])
                            op("gpsimd", lambda e: e.tensor_tensor(out=IMPv, in0=IMPv, in1=tv, op=ALU.add),
                               reads=[("imtmp", it)] + impres, writes=impres)
                        cdone[qc] = cdone.get(qc, 0) + 1
                    u = new_unit(fin, po=po)
                    return {"np": 127, "w": 512, "unit": u, "last": True,
                            "qk": [(0, 512, KCT[0:96, 0:127], QT[0:96, r, qc * 512:(qc + 1) * 512],
                                    (ident[:, 0:127], cmpmask[:, qc * 512:(qc + 1) * 512]),
                                    ["KCT", "const"] + [("QT", r, 4 * qc + j) for j in range(4)])],
                            "pv": [(j, VCA[0:127, 0:97], j * 97, 97, True, True, ["VCA", "Vones", "const"]) for j in range(4)]}

                def select_tile(T):
                    assert cdone.get(T // 4) == 4, "cmp units not finished"
                    sl = slice(T * 32, (T + 1) * 32)
                    op("vector", lambda e: e.tensor_tensor(out=IMP[:, sl], in0=IMP[:, sl], in1=impA[:, sl], op=ALU.mult),
                       reads=[("IMP", T), "const"], writes=[("IMP", T)])
                    op("vector", lambda e: e.tensor_tensor(out=IMP[:, sl], in0=IMP[:, sl], in1=impB[:, sl], op=ALU.add),
                       reads=[("IMP", T), "const"], writes=[("IMP", T)])
                    op("vector", lambda e: e.max(out=M8[:, T, :], in_=IMP[:, sl]), reads=[("IMP", T)], writes=[("M8", T)])
                    op("vector", lambda e: e.tensor_scalar(
                        out=stageQ[:, T, :, 64:96], in0=IMP[:, sl].unsqueeze(1).broadcast_to([128, 4, 32]),
                        scalar1=M8[:, T, 7:8], scalar2=NEGM, op0=ALU.is_lt, op1=ALU.mult),
                       reads=[("IMP", T), ("M8", T)], writes=[("stageQ", T)])

                def select_tr(T):
                    for r in range(4):
                        op("tensor", lambda e, r=r: e.transpose(out=pT[0][0:96, r * 128:(r + 1) * 128], in_=stageQ[:, T, r, 0:96], identity=ident[:]),
                           reads=[("stageQ", T), "const"], writes=[("pT", 0)])
                    op("vector", lambda e: e.tensor_copy(out=QT[64:96, :, T * 128:(T + 1) * 128],
                                                         in_=pT[0][64:96, 0:512].rearrange("p (r t) -> p r t", t=128)),
                       reads=[("pT", 0)], writes=[("QT", r, T) for r in range(4)])

                def super_fin(qb, branch, ndone, po):
                    def fin(u):
                        ri = cnt["rt"] % 2
                        cnt["rt"] += 1
                        fi = cnt["ft"] % 2
                        cnt["ft"] += 1
                        po4 = pO[po][:, 0:260].rearrange("p (r c) -> p r c", c=65)
                        op("vector", lambda e: e.reciprocal(out=rinv[ri][:].unsqueeze(2), in_=po4[:, :, 64:65]),
                           reads=[("pO", po)], writes=[("rinv", ri)])
                        Gv = G[:, qb, :].rearrange("p (r b) -> p r b", b=3)[:, :, branch]
                        op("vector", lambda e: e.tensor_tensor(out=scl[ri][:], in0=rinv[ri][:], in1=Gv, op=ALU.mult),
                           reads=[("rinv", ri), ("G", qb)], writes=[("scl", ri)])
                        op("vector", lambda e: e.tensor_tensor(out=fint[fi][:], in0=po4[:, :, 0:64],
                                                               in1=scl[ri][:].unsqueeze(2).broadcast_to([128, 4, 64]), op=ALU.mult),
                           reads=[("pO", po), ("scl", ri)], writes=[("fint", fi)])
                        av = acc[:, qb, :].rearrange("p (r d) -> p r d", d=64)
                        op("gpsimd", lambda e: e.tensor_tensor(out=av, in0=av, in1=fint[fi][:], op=ALU.add),
                           reads=[("fint", fi), ("acc", qb)], writes=[("acc", qb)])
                        if ndone is not None:
                            ndone(qb)
                    return fin

                cdone = {}
                strm = Streamer()
                for T in range(NT - 1, -1, -1):
                    qb = T
                    chunks = []
                    po = next_po()
                    for r in range(4):
                        if T % 4 == 3:
                            chunks.append(cmp_chunk(r, T // 4, 1 - po))
                        u = new_unit(super_fin(qb, 2, None, po) if r == 3 else nofin, po=po, ooff=65 * r)
                        kbs = list(range(max(0, qb - 4), qb + 1))
                        masks = {qb: (ident[:], cm[:])}
                        if qb - 4 >= 0:
                            masks[qb - 4] = (ident[:], am[:])
                        chunks += blocks_chunks(
                            u, QT[0:128, r, qb * 128:(qb + 1) * 128], ("QT", r, qb),
                            lambda kb: KT[0:128, 1, kb * 128:(kb + 1) * 128], lambda kb: ("KT", 1, kb),
                            lambda kb: Vaug[:, kb, 1, 0:65], lambda kb: ("V", kb),
                            kbs, qb, masks)
                    bg = []
                    if T % 4 == 2:
                        bg = [(lambda tt=tt: select_tile(tt)) for tt in range(4 * (T // 4) + 3, 4 * (T // 4) - 1, -1)]
                    if T % 4 == 1:
                        bg = [(lambda tt=tt: select_tr(tt)) for tt in range(4 * (T // 4) + 3, 4 * (T // 4) - 1, -1)]
                    strm.run(chunks, bg)
                strm.flush()

                tdone = {}

                def tile_done(T, g=g):
                    op("gpsimd", lambda e: e.tensor_tensor(out=zs[:, T, :], in0=zs[:, T, :], in1=acc[:, T, :], op=ALU.mult),
                       reads=[("acc", T), ("zs", T)], writes=[("zs", T)])
                    dst = og_d[T * 128:(T + 1) * 128, 256 * g:256 * g + 256]
                    op("sync", lambda e: e.dma_start(out=dst, in_=zs[:, T, :]), reads=[("zs", T)], writes=[("ogd", T)], dma="ogst")

                def unit_done(qb):
                    tdone[qb] = 4
                    tile_done(qb)

                strm = Streamer()
                for T in range(NT - 1, -1, -1):
                    qb = T
                    chunks = []
                    po = next_po()
                    for r in range(4):
                        u = new_unit(super_fin(qb, 1, unit_done, po) if r == 3 else nofin, po=po, ooff=65 * r)
                        chunks += blocks_chunks(
                            u, QT[0:128, r, qb * 128:(qb + 1) * 128], ("QT", r, qb),
                            lambda kb: KT[0:128, 0, kb * 128:(kb + 1) * 128], lambda kb: ("KT", 0, kb),
                            lambda kb: Vaug[:, kb, 0, 0:65], lambda kb: ("V", kb),
                            list(range(qb + 1)), qb, {qb: (ident[:], cm[:])})
                    bg = []
                    if nxt is not None:
                        n1, n2, n3 = nxt
                        free = []
                        if T + 1 < NT:
                            free = n1(T + 1)
                            bg.append(lambda T=T: tdone.get(T + 1) == 4 or _raise("tile not released"))
                            bg += n2(T + 1)
                        if T + 2 < NT:
                            bg += n3(T + 2)
                        strm.run(chunks, bg, free)
                    else:
                        strm.run(chunks, bg)
                strm.flush()
                if nxt is not None:
                    n1, n2, n3 = nxt
                    run_all(n3(1) + n1(0) + n2(0) + n3(0))
            chk("nsa")

        xin = din["x"]

        def chk(tag):
            if stop == tag:
                raise _Stop()

        try:
            chk("const")
            if 0 in layers:
                fox_preload()
                phase_norm(0, xin)
                chk("norm")
                fox_layer(lambda: (prefetch_out(0 if 1 in layers else 1, din["fox_w_out"], slot=1),
                                   nsa_preload_g0() if 1 in layers else None))
                chk("fox")
                phase_out(0 if 1 in layers else 1, xin, din["fox_w_out"], True)
            if 1 in layers:
                src = x1_d if 0 in layers else xin
                if 0 not in layers:
                    phase_norm(1, src)
                chk("norm")
                nsa_layer(lambda: prefetch_out(1, din["nsa_w_out"], slot=1))
                chk("nsa")
                phase_out(1, src, din["nsa_w_out"], False)
        except _Stop:
            pass
        P.emit()
    return nc


_CACHE = {}


def kernel(**inputs):
    if "nc" not in _CACHE:
        _CACHE["nc"] = build()
        _CACHE["consts"] = host_constants()
    nc = _CACHE["nc"]
    consts = _CACHE["consts"]
    x = np.ascontiguousarray(np.asarray(inputs["x"], dtype=np.float32))
    B = x.shape[0]
    shared = {}
    for name, shape in IN_SPECS:
        if name == "x":
            continue
        a = np.asarray(inputs[name], dtype=np.float32)
        shared[name] = np.ascontiguousarray(a.reshape(shape))
    shared.update(consts)
    in_maps = []
    for b in range(B):
        m = dict(shared)
        m["x"] = x[b]
        in_maps.append(m)
    res = run_bass_kernel_spmd(nc, in_maps, core_ids=list(range(B)))
    out = np.stack([np.asarray(r["out"], dtype=np.float32) for r in res.results], axis=0)
    return out
```

```python
import numpy as np
import ml_dtypes
from contextlib import ExitStack
import concourse.bass as bass
import concourse.mybir as mybir
from concourse.bass_utils import run_bass_kernel_spmd

F32 = mybir.dt.float32
BF = mybir.dt.bfloat16
AF = mybir.ActivationFunctionType
ALU = mybir.AluOpType

S = 2048
D = 1024
NT = 16
NEGM = -30000.0
EPS = 1e-6
FOX_IN = 4112
NSA_IN = 3632

ENGS = ["tensor", "vector", "scalar", "gpsimd", "sync"]


class _Rec:
    def __init__(self):
        self.call = None

    def __getattr__(self, name):
        def f(*a, **k):
            self.call = (name, a, k)
            return self
        return f


class Prog:
    def __init__(self, nc):
        self.nc = nc
        self.ops = {e: [] for e in ENGS}
        self.res = {}
        self.dma_cnt = {}
        self.final_tokens = []
        self.const_names = []

    def _st(self, r):
        st = self.res.get(r)
        if st is None:
            st = {"w": None, "r": {}}
            self.res[r] = st
        return st

    def op(self, eng, fn, reads=(), writes=(), dma=None, final=False, grp=None):
        rec = _Rec()
        fn(rec)
        assert rec.call is not None
        deps = []
        for r in reads:
            st = self._st(r)
            if st["w"] is not None:
                deps.append(st["w"])
            if isinstance(r, tuple) and r[0] in ("pA", "pT", "pS", "pO"):
                for (re_, _), rk in st["r"].items():
                    if re_ != eng:
                        deps.append(rk)
        for w in writes:
            st = self._st(w)
            if st["w"] is not None:
                deps.append(st["w"])
            deps.extend(st["r"].values())
        idx = len(self.ops[eng])
        o = {"call": rec.call, "deps": [], "signal": False, "dma": dma, "grp": grp}
        me = (eng, idx)
        for d in deps:
            if d == me:
                continue
            dop = self.ops[d[0]][d[1]]
            if grp is not None and dop["grp"] == grp:
                continue
            if dop["dma"] is None and d[0] == "tensor" and eng == "tensor":
                continue
            if d not in o["deps"]:
                o["deps"].append(d)
                dop["signal"] = True
        if dma is not None:
            o["signal"] = True
        self.ops[eng].append(o)
        for r in reads:
            self._st(r)["r"][(eng, dma)] = me
        for w in writes:
            st = self._st(w)
            st["w"] = me
            st["r"] = {}
        if final:
            self.final_tokens.append(me)
        return me

    def emit(self):
        nc = self.nc
        dma_keys = []
        for e in ENGS:
            cnt = 0
            for o in self.ops[e]:
                if o["dma"] is not None:
                    k = o["dma"]
                    if k not in self.dma_cnt:
                        self.dma_cnt[k] = 0
                        dma_keys.append(k)
                    self.dma_cnt[k] += 16
                    o["tok"] = (("dma", k), self.dma_cnt[k])
                elif o["signal"]:
                    cnt += 1
                    o["tok"] = (("eng", e), cnt)
        with ExitStack() as es:
            sems = {}
            for e in ENGS:
                sems[("eng", e)] = es.enter_context(nc.semaphore("s_" + e))
            for i, k in enumerate(dma_keys):
                sems[("dma", k)] = es.enter_context(nc.semaphore("d%d" % i))
            block = es.enter_context(nc.Block())
            prog = self

            def body(ename):
                def f(engobj):
                    waited = {}

                    def wait_for(d):
                        sk, val = prog.ops[d[0]][d[1]]["tok"]
                        if waited.get(sk, 0) >= val:
                            return
                        waited[sk] = val
                        engobj.wait_ge(sems[sk], val)

                    for o in prog.ops[ename]:
                        for d in o["deps"]:
                            wait_for(d)
                        name, a, k = o["call"]
                        ins = getattr(engobj, name)(*a, **k)
                        if o["signal"]:
                            sk, val = o["tok"]
                            ins.then_inc(sems[sk], 16 if o["dma"] is not None else 1)
                    if ename == "sync":
                        for d in prog.final_tokens:
                            wait_for(d)
                return f

            block.tensor(body("tensor"))
            block.vector(body("vector"))
            block.scalar(body("scalar"))
            block.gpsimd(body("gpsimd"))
            block.sync(body("sync"))


def host_constants():
    bf = ml_dtypes.bfloat16
    c = {}
    c["c_ident"] = np.eye(128, dtype=np.float32).astype(bf)
    kk = np.arange(128)[:, None]
    qq = np.arange(128)[None, :]
    c["c_tri"] = (kk <= qq).astype(np.float32)
    c["c_ones"] = np.ones((128, 128), np.float32)
    c["c_cm"] = np.where(kk <= qq, 0.0, NEGM).astype(np.float32).astype(bf)
    c["c_am"] = np.where(kk > qq, 0.0, NEGM).astype(np.float32).astype(bf)
    cc = np.arange(128)[:, None]
    tt = np.arange(S)[None, :]
    c["c_cmpmask"] = np.where((cc < 127) & (16 * cc + 31 <= tt), 0.0, NEGM).astype(np.float32).astype(bf)
    inv_freq = np.power(np.float32(500000.0), -np.arange(8, dtype=np.float32) * np.float32(2.0 / 16)).astype(np.float32)
    pos = (np.arange(NT)[None, :] * 128 + np.arange(128)[:, None]).astype(np.float32)
    ang = (pos[:, :, None] * inv_freq[None, None, :]).astype(np.float32)
    cos = np.cos(ang).astype(np.float32)
    sin = np.sin(ang).astype(np.float32)
    sc = np.array([0.125] * 4 + [1.0, 1.0], np.float32)[None, None, :, None]
    c["c_cos6"] = np.ascontiguousarray((cos[:, :, None, :] * sc).astype(np.float32).reshape(128, NT * 48))
    c["c_sin6"] = np.ascontiguousarray((sin[:, :, None, :] * sc).astype(np.float32).reshape(128, NT * 48))
    cpos = (np.arange(128) * 16 + 31).astype(np.float32)
    cang = (cpos[:, None] * inv_freq[None, :]).astype(np.float32)
    c["c_ccos"] = np.cos(cang).astype(np.float32)
    c["c_csin"] = np.sin(cang).astype(np.float32)
    jj = np.arange(32)[:, None]
    c["c_eall"] = ((np.arange(S)[None, :] // 64) == jj).astype(np.float32).astype(bf)
    p = np.arange(128)[:, None, None]
    T = np.arange(NT)[None, :, None]
    j = np.arange(32)[None, None, :]
    cur = 2 * T + (p >= 64)
    dd = j - cur
    forced = (j == 0) | (dd == 0) | (dd == -1)
    causal = dd <= 0
    A = ((~forced) & causal).astype(np.float32)
    B = np.where(forced, 1e6, np.where(causal, 0.0, -1.0)).astype(np.float32)
    c["c_impA"] = np.ascontiguousarray(np.broadcast_to(A, (128, NT, 32)).reshape(128, NT * 32))
    c["c_impB"] = np.ascontiguousarray(np.broadcast_to(B, (128, NT, 32)).reshape(128, NT * 32))
    ci = np.arange(128)[:, None] * 16
    sj = np.arange(32)[None, :] * 64
    ovl = ((ci < sj + 64) & (ci + 32 > sj) & (np.arange(128)[:, None] < 127)).astype(np.float32)
    c["c_ovl"] = ovl.astype(bf)
    return c


CONST_SPECS = [
    ("c_ident", [128, 128], BF), ("c_tri", [128, 128], F32), ("c_ones", [128, 128], F32),
    ("c_cm", [128, 128], BF), ("c_am", [128, 128], BF), ("c_cmpmask", [128, S], BF),
    ("c_cos6", [128, NT * 48], F32), ("c_sin6", [128, NT * 48], F32),
    ("c_ccos", [128, 8], F32), ("c_csin", [128, 8], F32), ("c_eall", [32, S], BF),
    ("c_impA", [128, NT * 32], F32), ("c_impB", [128, NT * 32], F32), ("c_ovl", [128, 32], BF),
]

IN_SPECS = [
    ("x", [S, D]), ("norm_g", [2, D]), ("fox_w_in", [D, FOX_IN]), ("fox_b_f", [1, 16]),
    ("fox_w_out", [D, D]), ("nsa_w_in", [D, NSA_IN]), ("nsa_pe_k", [32, 64]), ("nsa_w_ck1", [2048, 256]),
    ("nsa_w_ck2", [256, 64]), ("nsa_pe_v", [32, 64]), ("nsa_w_cv1", [2048, 256]), ("nsa_w_cv2", [256, 64]),
    ("nsa_w_out", [D, D]), ("final_g", [1, D]),
]


class _Stop(Exception):
    pass


def _raise(msg):
    raise RuntimeError(msg)


def build(layers=(0, 1), debug=False, stop=None):
    nc = bass.Bass("TRN2", target_bir_lowering=False)
    din = {}
    for name, shape in IN_SPECS:
        din[name] = nc.dram_tensor(name, shape, F32, kind="ExternalInput").ap()
    for name, shape, dt in CONST_SPECS:
        din[name] = nc.dram_tensor(name, shape, dt, kind="ExternalInput").ap()
    out_d = nc.dram_tensor("out", [S, D], F32, kind="ExternalOutput").ap()
    x1_d = nc.dram_tensor("x1s", [S, D], F32, kind="ExternalOutput" if debug else "Internal").ap()
    og_d = nc.dram_tensor("ogs", [S, D], BF, kind="Internal").ap()

    es = ExitStack()
    with es:
        def sb(name, shape, dt):
            return es.enter_context(nc.sbuf_tensor(name, shape, dt))

        def ps(name, shape, dt):
            return es.enter_context(nc.psum_tensor(name, shape, dt))

        hT = sb("hT", [128, 8, S], BF)
        wbuf = [sb("wbuf%d" % i, [128, 8192], BF) for i in range(2)]
        xt = [sb("xt%d" % i, [128, D], F32) for i in range(2)]
        ht = [sb("ht%d" % i, [128, D], BF) for i in range(2)]
        gbc = sb("gbc", [128, D], F32)
        stageQ = sb("stageQ", [128, NT, 4, 96], BF)
        stg = [sb("stg%d" % i, [128, 4, 96], BF) for i in range(2)]
        QT = sb("QT", [128, 4, S], BF)
        KT = sb("KT", [128, 4, S], BF)
        Vaug = sb("Vaug", [128, NT, 4, 65], BF)
        zs = sb("zs", [128, NT, 256], BF)
        acc = sb("acc", [128, NT, 256], F32)
        G = sb("G", [128, NT, 12], F32)
        PT = [sb("PT%d" % i, [128, 512], BF) for i in range(6)]
        ident = sb("ident", [128, 128], BF)
        tri = sb("tri", [128, 128], F32)
        ones = sb("ones", [128, 128], F32)
        cm = sb("cm", [128, 128], BF)
        am = sb("am", [128, 128], BF)
        cmpmask = sb("cmpmask", [128, S], BF)
        cos6 = sb("cos6", [128, NT, 6, 8], F32)
        sin6 = sb("sin6", [128, NT, 6, 8], F32)
        ccos = sb("ccos", [128, 8], F32)
        csin = sb("csin", [128, 8], F32)
        impA = sb("impA", [128, NT * 32], F32)
        impB = sb("impB", [128, NT * 32], F32)
        wf = sb("wf", [128, 8, 16], BF)
        bfb = sb("bfb", [128, 16], F32)
        Lf = sb("Lf", [128, NT, 16], F32)
        Lsum = sb("Lsum", [128, 16], F32)
        Cc = sb("Cc", [128, NT * 16], F32)
        R1 = sb("R1", [128, NT * 16], F32)
        HI = sb("HI", [128, NT * 16], BF)
        LO = sb("LO", [128, NT * 16], BF)
        LO2 = sb("LO2", [128, NT * 16], BF)
        QAUG = sb("QAUG", [128, NT, 16, 6], BF)
        KAUG = sb("KAUG", [128, NT, 16, 6], BF)
        ftmp = sb("ftmp", [128, 16], F32)
        IMP = sb("IMP", [128, NT * 32], F32)
        M8 = sb("M8", [128, NT, 8], F32)
        imtmp = [sb("imtmp%d" % i, [128, 128], F32) for i in range(2)]
        fint = [sb("fint%d" % i, [128, 4, 64], F32) for i in range(2)]
        rtmp = [sb("rtmp%d" % i, [128, 6, 8], F32) for i in range(4)]
        gtmp = sb("gtmp", [128, 12], F32)
        PE32 = sb("PE32", [32, 128], F32)
        PEB = sb("PEB", [32, 128], BF)
        peT = sb("peT", [128, 32], BF)
        W2 = sb("W2", [128, 2, 2, 64], BF)
        constkv = sb("constkv", [128, 2, 2], F32)
        HID = sb("HID", [128, 2, 128], BF)
        hpre = sb("hpre", [128, 128], F32)
        hexp = sb("hexp", [128, 128], F32)
        KCs = sb("KCs", [128, 64], BF)
        KCT = sb("KCT", [96, 128], BF)
        VCA = sb("VCA", [128, 97], BF)
        ctmp = [sb("ctmp%d" % i, [128, 8], F32) for i in range(4)]
        sgt = [sb("sgt%d" % i, [128, 256], F32) for i in range(2)]
        mhalf = sb("mhalf", [128, 1], F32)
        ss = [sb("ss%d" % i, [128, 1], F32) for i in range(2)]
        rstd = [sb("rstd%d" % i, [128, 1], F32) for i in range(2)]
        rinv = [sb("rinv%d" % i, [128, 4], F32) for i in range(2)]
        scl = [sb("scl%d" % i, [128, 4], F32) for i in range(2)]
        oT = [sb("oT%d" % i, [128, 8, 128], BF) for i in range(2)]
        pA = [ps("pA%d" % i, [128, 512], F32) for i in range(2)]
        pTf = [ps("pT%d" % i, [128, 512], F32) for i in range(2)]
        pS = [ps("pS%d" % i, [128, 512], F32) for i in range(2)]
        pO = [ps("pO%d" % i, [128, 512], F32) for i in range(2)]
        pT = [t[:].bitcast(BF) for t in pTf]
        SB = [("pS", 0), ("pS", 1), ("pT", 1)]
        sbank = {("pS", 0): pS[0], ("pS", 1): pS[1], ("pT", 1): pTf[1]}

        P = Prog(nc)
        import os
        _skip = set(os.environ.get("KSKIP", "").split(","))

        def op(eng, fn, tag=None, **kw):
            if tag is not None and tag in _skip:
                return None
            return P.op(eng, fn, **kw)
        cnt = {"xt": 0, "stg": 0, "pt": 0, "sb": 0, "po": 0, "rt": 0, "w": 0, "wl": 0, "sl": 0, "it": 0, "ft": 0}

        def cload(dst_ap, src_ap, eng="sync"):
            op(eng, lambda e: e.dma_start(out=dst_ap, in_=src_ap), writes=["const"], dma="const", grp="const")

        cload(ident[:], din["c_ident"])
        cload(tri[:], din["c_tri"])
        cload(ones[:], din["c_ones"])
        cload(cm[:], din["c_cm"])
        cload(am[:], din["c_am"])
        cload(cmpmask[:], din["c_cmpmask"])
        cload(cos6[:].rearrange("p a b c -> p (a b c)"), din["c_cos6"])
        cload(sin6[:].rearrange("p a b c -> p (a b c)"), din["c_sin6"])
        cload(ccos[:], din["c_ccos"])
        cload(csin[:], din["c_csin"])
        cload(impA[:], din["c_impA"])
        cload(impB[:], din["c_impB"])
        cload(bfb[:], din["fox_b_f"][0:1, :].broadcast_to([128, 16]))
        cload(VCA[:, 65:97], din["c_ovl"])
        cload(PE32[:, 0:64], din["nsa_pe_k"])
        cload(PE32[:, 64:128], din["nsa_pe_v"])
        op("vector", lambda e: e.memset(Vaug[:, :, :, 64:65], 1.0), writes=["Vones"])
        op("vector", lambda e: e.memset(VCA[:, 64:65], 1.0), writes=["Vones"])
        op("vector", lambda e: e.memset(QAUG[:, :, :, 3:6], 1.0), writes=["augones"])
        op("vector", lambda e: e.memset(KAUG[:, :, :, 0:3], 1.0), writes=["augones"])
        for i in range(2):
            op("gpsimd", lambda e, i=i: e.memset(wbuf[i][:], 0.0), writes=[("wbuf", i)])
        op("gpsimd", lambda e: e.memset(KCs[:], 0.0), writes=["KCs"])
        op("gpsimd", lambda e: e.memset(KCT[:], 0.0), writes=["KCT"])
        op("gpsimd", lambda e: e.memset(QT[:].rearrange("p a b -> p (a b)"), 0.0), writes=[("QT", r, T) for r in range(4) for T in range(NT)])
        op("gpsimd", lambda e: e.memset(KT[:].rearrange("p a b -> p (a b)"), 0.0), writes=[("KT", r, T) for r in range(4) for T in range(NT)])
        op("gpsimd", lambda e: e.memset(stageQ[:].rearrange("p a b c -> p (a b c)"), 0.0), writes=[("stageQ", T) for T in range(NT)])
        op("gpsimd", lambda e: e.memset(mhalf[:], -0.5), writes=["mhalf"])
        op("gpsimd", lambda e: e.memset(HID[:], 0.0), writes=["HID"])

        def w3(slot):
            return wbuf[slot][:].rearrange("p (a b) -> p a b", b=1024)

        def load_w_cols(slot, wsrc, segs):
            for (c0, n, d0) in segs:
                src = wsrc[:, c0:c0 + n].rearrange("(fc p) n -> p fc n", p=128)
                dst = w3(slot)[:, :, d0:d0 + n]
                op("gpsimd", lambda e, src=src, dst=dst: e.dma_start(out=dst, in_=src),
                   writes=[("wbuf", slot)], dma=("w", slot), grp=("wl", cnt["wl"]))
            cnt["wl"] += 1

        def rms_rstd(xtile, s, junk, junk_res=None, xres=None):
            op("vector", lambda e: e.scalar_tensor_tensor(out=junk, in0=xtile, scalar=1.0, in1=xtile, op0=ALU.mult, op1=ALU.mult,
                                                          accum_out=ss[s][:]),
               reads=(xres if xres is not None else [("xt", s)]), writes=[("ss", s), junk_res if junk_res is not None else ("ht", s)])
            op("vector", lambda e: e.tensor_scalar(out=rstd[s][:], in0=ss[s][:], scalar1=1.0 / D, scalar2=EPS,
                                                   op0=ALU.mult, op1=ALU.add),
               reads=[("ss", s)], writes=[("rstd", s)])
            op("gpsimd", lambda e: e.tensor_tensor(out=rstd[s][:], in0=rstd[s][:], in1=mhalf[:], op=ALU.pow),
               reads=[("rstd", s), "mhalf"], writes=[("rstd", s)])

        def silu_psum(zin, out_ap, reads_, writes_, n):
            i = cnt["sl"] % 2
            cnt["sl"] += 1
            t = sgt[i][:, 0:n]
            op("scalar", lambda e: e.activation(out=t, in_=zin, func=AF.Tanh, scale=0.5), reads=reads_, writes=[("sgt", i)])
            op("vector", lambda e: e.tensor_scalar(out=t, in0=t, scalar1=0.5, scalar2=0.5, op0=ALU.mult, op1=ALU.add), reads=[("sgt", i)], writes=[("sgt", i)])
            op("vector", lambda e: e.tensor_tensor(out=out_ap, in0=zin, in1=t, op=ALU.mult), reads=reads_ + [("sgt", i)], writes=writes_)

        def phase_norm(L, xsrc):
            op("sync", lambda e: e.dma_start(out=gbc[:], in_=din["norm_g"][L:L + 1, :].broadcast_to([128, D])),
               writes=["gbc"], dma="gbc")
            def xslot(T):
                s4 = T % 4
                return (acc[:, 4 * s4:4 * s4 + 4, :].rearrange("p a b -> p (a b)"), [("acc", 4 * s4 + i) for i in range(4)], s4)

            def nfront(T):
                s = T % 2
                xo, xres, s4 = xslot(T)
                op("sync", lambda e: e.dma_start(out=xo, in_=xsrc[T * 128:(T + 1) * 128, :]),
                   reads=[("xd", T)], writes=xres, dma=("xo", s4))
                rms_rstd(xo, s, oT[s][:].rearrange("p a b -> p (a b)"), junk_res=("oT", s), xres=xres)

            def nback(T):
                s = T % 2
                xo, xres, s4 = xslot(T)
                op("vector", lambda e: e.scalar_tensor_tensor(out=ht[s][:], in0=xo, scalar=rstd[s][:, 0:1],
                                                              in1=gbc[:], op0=ALU.mult, op1=ALU.mult),
                   reads=xres + [("rstd", s), "gbc"], writes=[("ht", s)])
                pi = T % 2
                for fc in range(8):
                    op("tensor", lambda e, fc=fc: e.transpose(
                        out=pT[pi][:, fc * 128:(fc + 1) * 128], in_=ht[s][:, fc * 128:(fc + 1) * 128], identity=ident[:]),
                       reads=[("ht", s), "const"], writes=[("pT", pi)])
                op("scalar", lambda e: e.copy(out=hT[:, :, T * 128:(T + 1) * 128],
                                              in_=pT[pi][:, :].rearrange("p (a b) -> p a b", b=128)),
                   reads=[("pT", pi)], writes=[("hT", T)])

            nfront(0)
            for T in range(NT - 1):
                nfront(T + 1)
                nback(T)
            nback(NT - 1)

        _once = set()

        def fox_preload():
            if "fox" in _once:
                return
            _once.add("fox")
            Wf = din["fox_w_in"]
            srcf = Wf[:, 3072:3088].rearrange("(fc p) n -> p fc n", p=128)
            op("gpsimd", lambda e: e.dma_start(out=wf[:], in_=srcf), writes=["wf"], dma="wf")
            load_w_cols(0, Wf, [(0, 256, 0), (1024, 256, 256), (2048, 256, 512), (3088, 256, 768)])

        def nsa_preload_w1():
            if "w1" in _once:
                return
            _once.add("w1")
            for kv, nm in enumerate(["nsa_w_ck1", "nsa_w_cv1"]):
                p0 = 64 * kv
                w1v = wbuf[1][p0:p0 + 64, :].rearrange("p (l n) -> p l n", n=256)
                srcw = din[nm].rearrange("(l d) n -> d l n", d=64)
                for half in range(2):
                    op("gpsimd", lambda e, w1v=w1v, srcw=srcw, half=half: e.dma_start(
                        out=w1v[:, half * 16:(half + 1) * 16, :], in_=srcw[:, half * 16:(half + 1) * 16, :]),
                       writes=[("wbuf", 1)], dma=("w", 1), grp="w1load")

        _pref = {}

        def nsa_preload_g0():
            if "nsag0" in _once:
                return
            _once.add("nsag0")
            segs0 = [(0, 256, 0), (1536, 64, 256), (2048, 64, 320), (1024, 64, 384), (1280, 64, 448), (1792, 64, 512),
                     (2304, 64, 576), (2560, 12, 640), (2608, 256, 656)]
            load_w_cols(0, din["nsa_w_in"], segs0)

        def prefetch_out(L, wsrc, slot=None):
            key = (L, id(wsrc))
            if key not in _pref:
                if slot is None:
                    slot = cnt["w"] % 2
                    cnt["w"] += 1
                load_w_cols(slot, wsrc, [(0, 512, 0), (512, 512, 512)])
                gsrc = din["norm_g"][1:2, :] if L == 0 else din["final_g"][0:1, :]
                op("sync", lambda e: e.dma_start(out=gbc[:], in_=gsrc.broadcast_to([128, D])), writes=["gbc"], dma="gbc")
                _pref[key] = slot
            return _pref[key]

        def phase_out(L, xsrc, wsrc, fuse_next_norm):
            slot = prefetch_out(L, wsrc)
            def front(T):
                s = T % 2
                s4 = T % 4
                xo = acc[:, 4 * s4:4 * s4 + 4, :].rearrange("p a b -> p (a b)")
                xres = [("acc", 4 * s4 + i) for i in range(4)]
                hh = zs[:, 4 * s4:4 * s4 + 4, :].rearrange("p a b -> p (a b)")
                hres = [("zs", 4 * s4 + i) for i in range(4)]
                op("sync", lambda e, s=s, T=T: e.dma_start(out=ht[s][:], in_=og_d[T * 128:(T + 1) * 128, :]),
                   reads=[("ogd", T)], writes=[("ht", s)], dma=("ot", s))
                op("sync", lambda e, T=T, xo=xo: e.dma_start(out=xo, in_=xsrc[T * 128:(T + 1) * 128, :]),
                   reads=[("xd", T)], writes=xres, dma=("xo", s4))

            def trog(T):
                s = T % 2
                pi = T % 2
                for fc in range(8):
                    op("tensor", lambda e, s=s, fc=fc, pi=pi: e.transpose(
                        out=pT[pi][:, fc * 128:(fc + 1) * 128], in_=ht[s][:, fc * 128:(fc + 1) * 128], identity=ident[:]),
                       reads=[("ht", s), "const"], writes=[("pT", pi)])
                op("scalar", lambda e, s=s, pi=pi: e.copy(out=oT[s][:], in_=pT[pi][:, :].rearrange("p (a b) -> p a b", b=128)),
                   reads=[("pT", pi)], writes=[("oT", s)])

            def mmo(T):
                s = T % 2
                for c in range(2):
                    for fc in range(8):
                        op("tensor", lambda e, s=s, c=c, fc=fc: e.matmul(
                            pA[c][:, :], lhsT=oT[s][:, fc, :], rhs=w3(slot)[:, fc, c * 512:(c + 1) * 512],
                            start=(fc == 0), stop=(fc == 7)),
                           reads=[("oT", s), ("wbuf", slot)], writes=[("pA", c)])

            def front_b(T):
                s4 = T % 4
                xo = acc[:, 4 * s4:4 * s4 + 4, :].rearrange("p a b -> p (a b)")
                xres = [("acc", 4 * s4 + i) for i in range(4)]
                for c in range(2):
                    op("vector", lambda e, c=c, xo=xo: e.tensor_tensor(
                        out=xo[:, c * 512:(c + 1) * 512], in0=pA[c][:, :], in1=xo[:, c * 512:(c + 1) * 512], op=ALU.add),
                       reads=[("pA", c)] + xres, writes=xres)

            def back(T):
                s = T % 2
                s4 = T % 4
                xo = acc[:, 4 * s4:4 * s4 + 4, :].rearrange("p a b -> p (a b)")
                xres = [("acc", 4 * s4 + i) for i in range(4)]
                hh = zs[:, 4 * s4:4 * s4 + 4, :].rearrange("p a b -> p (a b)")
                hres = [("zs", 4 * s4 + i) for i in range(4)]
                pi = T % 2
                if L == 0:
                    op("gpsimd", lambda e, T=T, xo=xo: e.dma_start(out=x1_d[T * 128:(T + 1) * 128, :], in_=xo),
                       reads=xres, writes=[("xd", T)], dma=("xst", s4), final=debug)
                sr = cnt["xt"] % 2
                cnt["xt"] += 1
                op("vector", lambda e, xo=xo, hh=hh, sr=sr: e.scalar_tensor_tensor(out=hh, in0=xo, scalar=1.0, in1=xo, op0=ALU.mult, op1=ALU.mult,
                                                                         accum_out=ss[sr][:]),
                   reads=xres, writes=[("ss", sr)] + hres)
                op("vector", lambda e, sr=sr: e.tensor_scalar(out=rstd[sr][:], in0=ss[sr][:], scalar1=1.0 / D, scalar2=EPS,
                                                             op0=ALU.mult, op1=ALU.add),
                   reads=[("ss", sr)], writes=[("rstd", sr)])
                op("gpsimd", lambda e, sr=sr: e.tensor_tensor(out=rstd[sr][:], in0=rstd[sr][:], in1=mhalf[:], op=ALU.pow),
                   reads=[("rstd", sr), "mhalf"], writes=[("rstd", sr)])
                if L == 0:
                    if fuse_next_norm:
                        op("vector", lambda e, xo=xo, hh=hh, sr=sr: e.scalar_tensor_tensor(out=hh, in0=xo, scalar=rstd[sr][:, 0:1],
                                                                                 in1=gbc[:], op0=ALU.mult, op1=ALU.mult),
                           reads=xres + [("rstd", sr), "gbc"], writes=hres)
                        for fc in range(8):
                            op("tensor", lambda e, hh=hh, fc=fc, pi=pi: e.transpose(
                                out=pT[pi][:, fc * 128:(fc + 1) * 128], in_=hh[:, fc * 128:(fc + 1) * 128], identity=ident[:]),
                               reads=hres + ["const"], writes=[("pT", pi)])
                        op("scalar", lambda e, T=T, pi=pi: e.copy(out=hT[:, :, T * 128:(T + 1) * 128],
                                                                 in_=pT[pi][:, :].rearrange("p (a b) -> p a b", b=128)),
                           reads=[("pT", pi)], writes=[("hT", T)])
                else:
                    op("vector", lambda e, xo=xo, sr=sr: e.scalar_tensor_tensor(out=xo, in0=xo, scalar=rstd[sr][:, 0:1],
                                                                      in1=gbc[:], op0=ALU.mult, op1=ALU.mult),
                       reads=xres + [("rstd", sr), "gbc"], writes=xres)
                    op("gpsimd", lambda e, T=T, xo=xo: e.dma_start(out=out_d[T * 128:(T + 1) * 128, :], in_=xo),
                       reads=xres, writes=[("outd", T)], dma=("xst", s4), final=True)


            front(0)
            front(1)
            trog(0)
            for T in range(NT):
                if T + 2 < NT:
                    front(T + 2)
                if T + 1 < NT:
                    trog(T + 1)
                mmo(T)
                if T >= 1:
                    back(T - 1)
                front_b(T)
            back(NT - 1)

        class Unit:
            pass

        LA = 2
        FILL = int(os.environ.get("KFILL", "0"))

        RING3 = [("pS", 0), ("pS", 1), ("pT", 1)]
        RING5 = [("pS", 0), ("pS", 1), ("pT", 1), ("pA", 0), ("pA", 1)]
        sbank[("pA", 0)] = pA[0]
        sbank[("pA", 1)] = pA[1]

        def emit_qk(ch, ring):
            si = cnt["sb"] % len(ring)
            cnt["sb"] += 1
            key = ring[si]
            bank = sbank[key]
            npart = ch["np"]
            for (coff, ncols, lhsT, rhs, mask, rd) in ch["qk"]:
                op("tensor", lambda e, coff=coff, ncols=ncols, lhsT=lhsT, rhs=rhs, mask=mask:
                   e.matmul(bank[0:npart, coff:coff + ncols], lhsT=lhsT, rhs=rhs, start=True, stop=(mask is None)),
                   reads=rd, writes=[key])
                if mask is not None:
                    op("tensor", lambda e, coff=coff, ncols=ncols, mask=mask:
                       e.matmul(bank[0:npart, coff:coff + ncols], lhsT=mask[0], rhs=mask[1], start=False, stop=True),
                       reads=["const"], writes=[key])
            if FILL > 0:
                op("tensor", lambda e: e.matmul(pTf[1][:, 0:FILL], lhsT=ident[:], rhs=cmpmask[:, 0:FILL], start=True, stop=True),
                   reads=["const"], writes=[("pT", 1)])
            pi = cnt["pt"] % 6
            cnt["pt"] += 1
            ch["pt"] = pi
            w = ch["w"]
            op("scalar", lambda e, pi=pi, w=w: e.activation(out=PT[pi][0:npart, 0:w], in_=bank[0:npart, 0:w], func=AF.Exp),
               reads=[key], writes=[("PT", pi)])

        def emit_pv(ch):
            pi = ch["pt"]
            npart = ch["np"]
            u = ch["unit"]
            for (j, rhs, ooff, on, st, sp, rd) in ch["pv"]:
                op("tensor", lambda e, j=j, rhs=rhs, ooff=ooff, on=on, st=st, sp=sp, pi=pi:
                   e.matmul(pO[u.po][:, ooff:ooff + on], lhsT=PT[pi][0:npart, j * 128:(j + 1) * 128], rhs=rhs,
                            start=st, stop=sp),
                   reads=[("PT", pi)] + rd, writes=[("pO", u.po)])
            if ch.get("last"):
                u.fin(u)

        class Streamer:
            def __init__(self, deep=False):
                self.pending = []
                self.ring = RING5 if deep else RING3
                self.la = 4 if deep else LA

            def push(self, ch):
                emit_qk(ch, self.ring)
                self.pending.append(ch)
                if len(self.pending) > self.la:
                    emit_pv(self.pending.pop(0))

            def flush(self):
                while self.pending:
                    emit_pv(self.pending.pop(0))

            def run(self, chunks, bg=(), free=()):
                items = list(free) + list(bg)
                nfree = len(free)
                done = 0
                n = len(chunks)
                for i, ch in enumerate(chunks):
                    self.push(ch)
                    tgt = (len(items) * (i + 1)) // max(1, n)
                    if i + 1 <= self.la:
                        tgt = min(tgt, nfree)
                    while done < tgt:
                        items[done]()
                        done += 1
                while done < len(items):
                    items[done]()
                    done += 1

        def attn_stream(chunks):
            st_ = Streamer()
            st_.run(chunks)
            st_.flush()

        def new_unit(fin, po=None, ooff=0):
            u = Unit()
            if po is None:
                po = cnt["po"] % 2
                cnt["po"] += 1
            u.po = po
            u.ooff = ooff
            u.fin = fin
            return u

        def next_po():
            po = cnt["po"] % 2
            cnt["po"] += 1
            return po

        def nofin(u):
            pass

        def blocks_chunks(u, qTap, qres, kT_fn, kres_fn, v_fn, vres_fn, kbs, qb, masks):
            chs = []
            for c0 in range(0, len(kbs), 4):
                sub = kbs[c0:c0 + 4]
                ch = {"np": 128, "w": len(sub) * 128, "unit": u, "qk": [], "pv": []}
                for j, kb in enumerate(sub):
                    ch["qk"].append((j * 128, 128, kT_fn(kb), qTap, masks.get(kb), [kres_fn(kb), qres, "const"]))
                    ch["pv"].append((j, v_fn(kb), u.ooff, 65, kb == kbs[0], kb == kbs[-1], [vres_fn(kb), "Vones"]))
                chs.append(ch)
            chs[-1]["last"] = True
            return chs

        def fox_layer(out_pref=None):
            W = din["fox_w_in"]
            fox_preload()
            for T in range(NT):
                a = T % 2
                for fc in range(8):
                    op("tensor", lambda e, T=T, fc=fc, a=a: e.matmul(pA[a][:, 0:16], lhsT=hT[:, fc, T * 128:(T + 1) * 128],
                                                                    rhs=wf[:, fc, :], start=(fc == 0), stop=(fc == 7)),
                       reads=[("hT", T), "wf"], writes=[("pA", a)])
                op("vector", lambda e, a=a: e.tensor_tensor(out=ftmp[:], in0=pA[a][:, 0:16], in1=bfb[:], op=ALU.add),
                   reads=[("pA", a), "const"], writes=["ftmp"])
                op("scalar", lambda e: e.activation(out=ftmp[:], in_=ftmp[:], func=AF.Exp, scale=-1.0),
                   reads=["ftmp"], writes=["ftmp"])
                op("scalar", lambda e, T=T: e.activation(out=Lf[:, T, :], in_=ftmp[:], func=AF.Ln, bias=1.0),
                   reads=["ftmp"], writes=[("Lf", T)])
            for T in range(NT):
                a = T % 2
                op("tensor", lambda e, T=T, a=a: e.matmul(pA[a][:, 0:16], lhsT=tri[:], rhs=Lf[:, T, :], start=True, stop=(T == 0)),
                   reads=[("Lf", T), "const"], writes=[("pA", a)])
                if T > 0:
                    if T == 1:
                        op("vector", lambda e: e.tensor_copy(out=Lsum[:], in_=Lf[:, 0, :]), reads=[("Lf", 0)], writes=["Lsum"])
                    else:
                        op("vector", lambda e, T=T: e.tensor_tensor(out=Lsum[:], in0=Lsum[:], in1=Lf[:, T - 1, :], op=ALU.add),
                           reads=["Lsum", ("Lf", T - 1)], writes=["Lsum"])
                    op("tensor", lambda e, a=a: e.matmul(pA[a][:, 0:16], lhsT=ones[:], rhs=Lsum[:], start=False, stop=True),
                       reads=["Lsum", "const"], writes=[("pA", a)])
                op("vector", lambda e, T=T, a=a: e.tensor_scalar(out=Cc[:, T * 16:(T + 1) * 16], in0=pA[a][:, 0:16], scalar1=-1.0,
                                                                scalar2=None, op0=ALU.mult),
                   reads=[("pA", a)], writes=["Cc"])
            v = "vector"
            op(v, lambda e: e.tensor_copy(out=HI[:], in_=Cc[:]), reads=["Cc"], writes=["HI"])
            op(v, lambda e: e.tensor_tensor(out=R1[:], in0=Cc[:], in1=HI[:], op=ALU.subtract), reads=["Cc", "HI"], writes=["R1"])
            op(v, lambda e: e.tensor_copy(out=LO[:], in_=R1[:]), reads=["R1"], writes=["LO"])
            op(v, lambda e: e.tensor_tensor(out=R1[:], in0=R1[:], in1=LO[:], op=ALU.subtract), reads=["R1", "LO"], writes=["R1"])
            op(v, lambda e: e.tensor_copy(out=LO2[:], in_=R1[:]), reads=["R1"], writes=["LO2"])
            for i, src_t in enumerate([HI, LO, LO2]):
                sv = src_t[:].rearrange("p (a b) -> p a b", b=16)
                op(v, lambda e, i=i, sv=sv: e.tensor_copy(out=QAUG[:, :, :, i], in_=sv), reads=["HI", "LO", "LO2", "augones"], writes=["QAUG"])
                op(v, lambda e, i=i, sv=sv: e.tensor_scalar(out=KAUG[:, :, :, 3 + i], in0=sv, scalar1=-1.0, scalar2=None, op0=ALU.mult),
                   reads=["HI", "LO", "LO2", "augones"], writes=["KAUG"])

            chk("decay")
            def fox_stages(u_, slot, alt=False):

                def bk(T):
                    if alt and T % 2 == 1:
                        return ((("pS", 0), pS[0]), (("pS", 1), pS[1]))
                    return ((("pA", 0), pA[0]), (("pA", 1), pA[1]))
                def S1(T):
                    def mk(c, h):
                        def f():
                            for fc in range(h, h + 1):
                                op("tensor", lambda e, fc=fc: e.matmul(
                                    bk(T)[c][1][:, :], lhsT=hT[:, fc, T * 128:(T + 1) * 128], rhs=w3(slot)[:, fc, c * 512:(c + 1) * 512],
                                    start=(fc == 0), stop=(fc == 7)),
                                   reads=[("hT", T), ("wbuf", slot)], writes=[bk(T)[c][0]])
                        return f
                    return [mk(c, h) for c in range(2) for h in range(8)]

                def S2(T):
                    s = T % 2
                    i = T % 2
                    t = sgt[i][:, 0:256]

                    def a():
                        op("scalar", lambda e: e.activation(out=t, in_=bk(T)[1][1][:, 256:512], func=AF.Tanh, scale=0.5), reads=[bk(T)[1][0]], writes=[("sgt", i)])

                    def b():
                        op("vector", lambda e: e.tensor_scalar(out=stageQ[:, T, :, 0:64], in0=bk(T)[0][1][:, 0:256].rearrange("p (r d) -> p r d", d=64),
                                                               scalar1=0.125, scalar2=None, op0=ALU.mult),
                           reads=[bk(T)[0][0]], writes=[("stageQ", T)])
                        op("vector", lambda e: e.tensor_copy(out=stg[s][:, :, 0:64], in_=bk(T)[0][1][:, 256:512].rearrange("p (r d) -> p r d", d=64)),
                           reads=[bk(T)[0][0]], writes=[("stg", s)])

                    def c():
                        op("gpsimd", lambda e: e.tensor_copy(out=stageQ[:, T, :, 64:70], in_=QAUG[:, T, 4 * u_:4 * u_ + 4, :]),
                           reads=["QAUG"], writes=[("stageQ", T)])
                        op("gpsimd", lambda e: e.tensor_copy(out=stg[s][:, :, 64:70], in_=KAUG[:, T, 4 * u_:4 * u_ + 4, :]),
                           reads=["KAUG"], writes=[("stg", s)])

                    def d():
                        op("vector", lambda e: e.tensor_copy(out=Vaug[:, T, :, 0:64], in_=bk(T)[1][1][:, 0:256].rearrange("p (r d) -> p r d", d=64)),
                           reads=[bk(T)[1][0]], writes=[("V", T)])
                        op("vector", lambda e: e.tensor_scalar(out=t, in0=t, scalar1=0.5, scalar2=0.5, op0=ALU.mult, op1=ALU.add), reads=[("sgt", i)], writes=[("sgt", i)])
                        op("vector", lambda e: e.tensor_tensor(out=zs[:, T, :], in0=bk(T)[1][1][:, 256:512], in1=t, op=ALU.mult),
                           reads=[bk(T)[1][0], ("sgt", i)], writes=[("zs", T)])
                    return [a, b, c, d]

                def S3(T):
                    s = T % 2

                    def tr():
                        for r in range(4):
                            op("tensor", lambda e, r=r: e.transpose(out=pT[0][0:70, r * 128:(r + 1) * 128], in_=stageQ[:, T, r, 0:70], identity=ident[:]),
                               reads=[("stageQ", T), "const"], writes=[("pT", 0)])

                    def tr2():
                        for r in range(4):
                            op("tensor", lambda e, r=r: e.transpose(out=pT[0][0:70, 512 + r * 128:512 + (r + 1) * 128], in_=stg[s][:, r, 0:70], identity=ident[:]),
                               reads=[("stg", s), "const"], writes=[("pT", 0)])

                    def ev():
                        op("vector", lambda e: e.tensor_copy(out=QT[0:70, :, T * 128:(T + 1) * 128],
                                                             in_=pT[0][0:70, 0:512].rearrange("p (r t) -> p r t", t=128)),
                           reads=[("pT", 0)], writes=[("QT", r, T) for r in range(4)])

                    def ev2():
                        op("vector", lambda e: e.tensor_copy(out=KT[0:70, :, T * 128:(T + 1) * 128],
                                                             in_=pT[0][0:70, 512:1024].rearrange("p (r t) -> p r t", t=128)),
                           reads=[("pT", 0)], writes=[("KT", r, T) for r in range(4)])
                    return [tr, tr2, ev, ev2]
                return S1, S2, S3

            def run_all(fs):
                for f in fs:
                    f()

            chk("decay")
            slots = [0, 1, 0, 1]
            wsegs = lambda u_: [(256 * u_, 256, 0), (1024 + 256 * u_, 256, 256), (2048 + 256 * u_, 256, 512), (3088 + 256 * u_, 256, 768)]
            fox_preload()
            S1, S2, S3 = fox_stages(0, slots[0], alt=True)
            run_all(S1(0))
            for T in range(NT):
                if T + 1 < NT:
                    run_all(S1(T + 1))
                run_all(S2(T))
                if T >= 1:
                    run_all(S3(T - 1))
            run_all(S3(NT - 1))
            for u_ in range(4):
                nxt = None
                if u_ + 1 < 4:
                    load_w_cols(slots[u_ + 1], W, wsegs(u_ + 1))
                    nxt = fox_stages(u_ + 1, slots[u_ + 1])
                elif out_pref is not None:
                    out_pref()
                tdone = {}

                def tile_done(T, u_=u_):
                    dst = og_d[T * 128:(T + 1) * 128, 256 * u_:256 * u_ + 256]
                    op("sync", lambda e: e.dma_start(out=dst, in_=zs[:, T, :]), reads=[("zs", T)], writes=[("ogd", T)], dma="ogst")

                strm = Streamer(deep=(nxt is None))
                for T in range(NT - 1, -1, -1):
                    chunks = []
                    qb = T
                    po = next_po()

                    def fin(u, qb=qb, po=po):
                        ri = cnt["rt"] % 2
                        cnt["rt"] += 1
                        fi = cnt["ft"] % 2
                        cnt["ft"] += 1
                        po4 = pO[po][:, 0:260].rearrange("p (r c) -> p r c", c=65)
                        op("vector", lambda e: e.reciprocal(out=rinv[ri][:].unsqueeze(2), in_=po4[:, :, 64:65]),
                           reads=[("pO", po)], writes=[("rinv", ri)])
                        op("vector", lambda e: e.tensor_tensor(out=fint[fi][:], in0=po4[:, :, 0:64],
                                                               in1=rinv[ri][:].unsqueeze(2).broadcast_to([128, 4, 64]), op=ALU.mult),
                           reads=[("pO", po), ("rinv", ri)], writes=[("fint", fi)])
                        zv = zs[:, qb, :].rearrange("p (r d) -> p r d", d=64)
                        op("gpsimd", lambda e: e.tensor_tensor(out=zv, in0=zv, in1=fint[fi][:], op=ALU.mult),
                           reads=[("fint", fi), ("zs", qb)], writes=[("zs", qb)])
                        tdone[qb] = 4
                        tile_done(qb)
                    for r in range(4):
                        u = new_unit(fin if r == 3 else nofin, po=po, ooff=65 * r)
                        qTap = QT[0:128, r, qb * 128:(qb + 1) * 128]
                        chunks += blocks_chunks(
                            u, qTap, ("QT", r, qb),
                            lambda kb, r=r: KT[0:128, r, kb * 128:(kb + 1) * 128], lambda kb, r=r: ("KT", r, kb),
                            lambda kb, r=r: Vaug[:, kb, r, 0:65], lambda kb: ("V", kb),
                            list(range(qb + 1)), qb, {qb: (ident[:], cm[:])})
                    bg = []
                    if nxt is not None:
                        n1, n2, n3 = nxt
                        free = []
                        if T + 1 < NT:
                            free = n1(T + 1)
                            bg.append(lambda T=T: tdone.get(T + 1) == 4 or _raise("tile not released"))
                            bg += n2(T + 1)
                        if T + 2 < NT:
                            bg += n3(T + 2)
                        strm.run(chunks, bg, free)
                    else:
                        strm.run(chunks, bg)
                strm.flush()
                if nxt is not None:
                    n1, n2, n3 = nxt
                    run_all(n3(1) + n1(0) + n2(0) + n3(0))
            chk("fox")

        def nsa_layer(out_pref=None):
            def run_all(fs):
                for f in fs:
                    f()

            W = din["nsa_w_in"]
            op("vector", lambda e: e.tensor_copy(out=PEB[:], in_=PE32[:]), reads=["const"], writes=["PEB"])
            op("tensor", lambda e: e.transpose(out=pT[0][:, 0:32], in_=PEB[:], identity=ident[0:32, 0:32]),
               reads=["PEB", "const"], writes=[("pT", 0)])
            op("vector", lambda e: e.tensor_copy(out=peT[:], in_=pT[0][:, 0:32]), reads=[("pT", 0)], writes=["peT"])
            for kv, nm in enumerate(["nsa_w_ck2", "nsa_w_cv2"]):
                src = din[nm].rearrange("(c p) d -> p c d", p=128)
                op("gpsimd", lambda e, kv=kv, src=src: e.dma_start(out=W2[:, kv, :, :], in_=src), writes=["W2"], dma="w2")
            op("sync", lambda e: e.dma_start(out=KT[64:96, 0, :], in_=din["c_eall"]),
               writes=[("KT", 0, T) for T in range(NT)], dma="eall")
            op("gpsimd", lambda e: e.memset(KT[64:96, 1, :], 0.0), writes=[("KT", 1, T) for T in range(NT)])

            def nsa_segs(g):
                return [(256 * g, 256, 0), (1536 + 64 * g, 64, 256), (2048 + 64 * g, 64, 320), (1024 + 64 * g, 64, 384),
                        (1280 + 64 * g, 64, 448), (1792 + 64 * g, 64, 512), (2304 + 64 * g, 64, 576),
                        (2560 + 12 * g, 12, 640), (2608 + 256 * g, 256, 656)]

            WS = 0
            W1S = 1

            def nsa_stages(g, alt=False):

                def bk(T):
                    if alt and T % 2 == 1:
                        return ((("pS", 0), pS[0]), (("pS", 1), pS[1]))
                    return ((("pA", 0), pA[0]), (("pA", 1), pA[1]))
                def S1(T):
                    def mk(c, c0, nn, h):
                        def f():
                            for fc in range(h, h + 1):
                                op("tensor", lambda e, fc=fc: e.matmul(
                                    bk(T)[c][1][:, 0:nn], lhsT=hT[:, fc, T * 128:(T + 1) * 128], rhs=w3(WS)[:, fc, c0:c0 + nn],
                                    start=(fc == 0), stop=(fc == 7)),
                                   reads=[("hT", T), ("wbuf", WS)], writes=[bk(T)[c][0]])
                        return f
                    return [mk(c, c0, nn, h) for (c, c0, nn) in ((0, 0, 512), (1, 512, 400)) for h in range(8)]

                def S2(T):
                    k0, k1 = bk(T)[0][0], bk(T)[1][0]
                    b0, b1 = bk(T)[0][1], bk(T)[1][1]
                    s = T % 2
                    sg = stg[s][:].rearrange("p a b -> p (a b)")
                    p6 = b0[:, 0:384].rearrange("p (s d) -> p s d", d=64)
                    x1 = p6[:, :, 0:8]
                    x2 = p6[:, :, 8:16]
                    i = T % 2
                    t = sgt[i][:, 0:256]
                    sg3 = sg[:, 0:128].rearrange("p (s d) -> p s d", d=64)

                    def a():
                        op("vector", lambda e: e.tensor_tensor(out=rtmp[0][:], in0=x1, in1=cos6[:, T, :, :], op=ALU.mult),
                           reads=[k0, "const"], writes=[("rtmp", 0)])
                        op("vector", lambda e: e.tensor_tensor(out=rtmp[1][:], in0=x2, in1=sin6[:, T, :, :], op=ALU.mult),
                           reads=[k0, "const"], writes=[("rtmp", 1)])
                        op("vector", lambda e: e.tensor_tensor(out=rtmp[2][:], in0=x1, in1=sin6[:, T, :, :], op=ALU.mult),
                           reads=[k0, "const"], writes=[("rtmp", 2)])
                        op("vector", lambda e: e.tensor_tensor(out=rtmp[3][:], in0=x2, in1=cos6[:, T, :, :], op=ALU.mult),
                           reads=[k0, "const"], writes=[("rtmp", 3)])

                    def b():
                        op("vector", lambda e: e.tensor_scalar(out=stageQ[:, T, :, 16:64], in0=p6[:, 0:4, 16:64], scalar1=0.125, scalar2=None, op0=ALU.mult),
                           reads=[k0], writes=[("stageQ", T)])
                        op("vector", lambda e: e.tensor_copy(out=sg3[:, :, 16:64], in_=p6[:, 4:6, 16:64]),
                           reads=[k0], writes=[("stg", s)])
                        op("vector", lambda e: e.tensor_copy(out=sg[:, 128:256], in_=b0[:, 384:512]),
                           reads=[k0], writes=[("stg", s)])

                    def c():
                        op("gpsimd", lambda e: e.tensor_tensor(out=stageQ[:, T, :, 0:8], in0=rtmp[0][:, 0:4, :], in1=rtmp[1][:, 0:4, :], op=ALU.subtract),
                           reads=[("rtmp", 0), ("rtmp", 1)], writes=[("stageQ", T)])
                        op("gpsimd", lambda e: e.tensor_tensor(out=stageQ[:, T, :, 8:16], in0=rtmp[2][:, 0:4, :], in1=rtmp[3][:, 0:4, :], op=ALU.add),
                           reads=[("rtmp", 2), ("rtmp", 3)], writes=[("stageQ", T)])
                        op("gpsimd", lambda e: e.tensor_tensor(out=sg3[:, :, 0:8], in0=rtmp[0][:, 4:6, :], in1=rtmp[1][:, 4:6, :], op=ALU.subtract),
                           reads=[("rtmp", 0), ("rtmp", 1)], writes=[("stg", s)])
                        op("gpsimd", lambda e: e.tensor_tensor(out=sg3[:, :, 8:16], in0=rtmp[2][:, 4:6, :], in1=rtmp[3][:, 4:6, :], op=ALU.add),
                           reads=[("rtmp", 2), ("rtmp", 3)], writes=[("stg", s)])

                    def d():
                        op("scalar", lambda e: e.activation(out=t, in_=b1[:, 144:400], func=AF.Tanh, scale=0.5), reads=[k1], writes=[("sgt", i)])
                        op("scalar", lambda e: e.activation(out=gtmp[:], in_=b1[:, 128:140], func=AF.Exp, scale=-1.0),
                           reads=[k1], writes=["gtmp"])

                    def e_():
                        op("vector", lambda e: e.tensor_copy(out=Vaug[:, T, 0:2, 0:64], in_=b1[:, 0:128].rearrange("p (s d) -> p s d", d=64)),
                           reads=[k1], writes=[("V", T)])
                        op("vector", lambda e: e.tensor_scalar(out=t, in0=t, scalar1=0.5, scalar2=0.5, op0=ALU.mult, op1=ALU.add), reads=[("sgt", i)], writes=[("sgt", i)])
                        op("vector", lambda e: e.tensor_tensor(out=zs[:, T, :], in0=b1[:, 144:400], in1=t, op=ALU.mult),
                           reads=[k1, ("sgt", i)], writes=[("zs", T)])
                        op("vector", lambda e: e.tensor_scalar(out=gtmp[:], in0=gtmp[:], scalar1=1.0, scalar2=None, op0=ALU.add),
                           reads=["gtmp"], writes=["gtmp"])
                        op("vector", lambda e: e.reciprocal(out=G[:, T, :], in_=gtmp[:]), reads=["gtmp"], writes=[("G", T)])
                    return [a, d, b, c, e_]

                def S3(T):
                    s = T % 2
                    sg = stg[s][:].rearrange("p a b -> p (a b)")

                    def tr():
                        for r in range(4):
                            op("tensor", lambda e, r=r: e.transpose(out=pT[0][0:96, r * 128:(r + 1) * 128], in_=stageQ[:, T, r, 0:96], identity=ident[:]),
                               reads=[("stageQ", T), "const"], writes=[("pT", 0)])
                        op("tensor", lambda e: e.transpose(out=pT[0][:, 512:640], in_=sg[:, 0:128], identity=ident[:]),
                           reads=[("stg", s), "const"], writes=[("pT", 0)])
                        op("tensor", lambda e: e.transpose(out=pT[0][:, 640:768], in_=sg[:, 128:256], identity=ident[:]),
                           reads=[("stg", s), "const"], writes=[("pT", 0)])

                    def ev():
                        op("vector", lambda e: e.tensor_copy(out=QT[0:64, :, T * 128:(T + 1) * 128],
                                                             in_=pT[0][0:64, 0:512].rearrange("p (r t) -> p r t", t=128)),
                           reads=[("pT", 0)], writes=[("QT", r, T) for r in range(4)])
                        op("vector", lambda e: e.tensor_copy(out=KT[0:64, 0, T * 128:(T + 1) * 128], in_=pT[0][0:64, 512:640]),
                           reads=[("pT", 0)], writes=[("KT", 0, T)])

                    def ev2():
                        op("vector", lambda e: e.tensor_copy(out=KT[0:64, 1, T * 128:(T + 1) * 128], in_=pT[0][64:128, 512:640]),
                           reads=[("pT", 0)], writes=[("KT", 1, T)])
                        op("vector", lambda e: e.tensor_copy(out=KT[:, 2, :].rearrange("p (b c) -> p b c", c=128)[:, :, 8 * T:8 * T + 8],
                                                             in_=pT[0][:, 640:768].rearrange("p (a b) -> p b a", b=16)),
                           reads=[("pT", 0)], writes=[("KT", 2, T)])
                    return [tr, ev, ev2]
                return S1, S2, S3

            w1view = {}
            for kv, nm in enumerate(["nsa_w_ck1", "nsa_w_cv1"]):
                p0 = 64 * kv
                w1v = wbuf[W1S][p0:p0 + 64, :].rearrange("p (l n) -> p l n", n=256)
                w1view[kv] = w1v
            nsa_preload_w1()
            for kv in range(2):
                p0 = 64 * kv
                for nch in range(2):
                    for l in range(32):
                        op("tensor", lambda e, nch=nch, l=l, kv=kv, p0=p0: e.matmul(
                            pA[0][:, 2 * kv + nch:2 * kv + nch + 1], lhsT=w1view[kv][:, l, nch * 128:(nch + 1) * 128],
                            rhs=peT[p0:p0 + 64, l:l + 1], start=(l == 0), stop=(l == 31)),
                           reads=[("wbuf", W1S), "peT"], writes=[("pA", 0)])
                op("vector", lambda e, kv=kv: e.tensor_copy(out=constkv[:, kv, :], in_=pA[0][:, 2 * kv:2 * kv + 2]),
                   reads=[("pA", 0)], writes=["constkv"])

            nsa_preload_g0()
            S1, S2, S3 = nsa_stages(0, alt=True)
            run_all(S1(0))
            for T in range(NT):
                if T + 1 < NT:
                    run_all(S1(T + 1))
                run_all(S2(T))
                if T >= 1:
                    run_all(S3(T - 1))
            run_all(S3(NT - 1))

            for g in range(4):
                nxt = None
                if g + 1 < 4:
                    load_w_cols(WS, W, nsa_segs(g + 1))
                    nxt = nsa_stages(g + 1)
                kraw_res = [("KT", 2, T) for T in range(NT)]
                for kv in range(2):
                    p0 = 64 * kv
                    w1v = w1view[kv]
                    raw = KT[p0:p0 + 64, 2, :].rearrange("p (b c) -> p b c", c=128)
                    for nch in range(2):
                        for l in range(32):
                            rhs = raw[:, l, 0:127] if l < 16 else raw[:, l - 16, 1:128]
                            op("tensor", lambda e, nch=nch, l=l, w1v=w1v, rhs=rhs: e.matmul(
                                pA[nch][:, 0:127], lhsT=w1v[:, l, nch * 128:(nch + 1) * 128], rhs=rhs,
                                start=(l == 0), stop=(l == 31)),
                               reads=[("wbuf", W1S)] + kraw_res, writes=[("pA", nch)])
                        op("scalar", lambda e, nch=nch, kv=kv: e.activation(out=hpre[:, 0:127], in_=pA[nch][:, 0:127], func=AF.Identity,
                                                                            bias=constkv[:, kv, nch:nch + 1]),
                           reads=[("pA", nch), "constkv"], writes=["hpre"])
                        op("scalar", lambda e: e.activation(out=hexp[:, 0:127], in_=hpre[:, 0:127], func=AF.Tanh, scale=0.5),
                           reads=["hpre"], writes=["hexp"])
                        op("vector", lambda e: e.tensor_scalar(out=hexp[:, 0:127], in0=hexp[:, 0:127], scalar1=0.5, scalar2=0.5, op0=ALU.mult, op1=ALU.add),
                           reads=["hexp"], writes=["hexp"])
                        op("vector", lambda e, nch=nch: e.tensor_tensor(out=HID[:, nch, 0:127], in0=hpre[:, 0:127], in1=hexp[:, 0:127], op=ALU.mult),
                           reads=["hpre", "hexp"], writes=["HID"])
                    for nch in range(2):
                        op("tensor", lambda e, nch=nch, kv=kv: e.matmul(pA[0][0:127, 0:64], lhsT=HID[:, nch, 0:127], rhs=W2[:, kv, nch, :],
                                                                      start=(nch == 0), stop=(nch == 1)),
                           reads=["HID", "W2"], writes=[("pA", 0)])
                    if kv == 0:
                        kx1 = pA[0][0:127, 0:8]
                        kx2 = pA[0][0:127, 8:16]
                        rd = [("pA", 0), "const"]
                        op("vector", lambda e: e.tensor_tensor(out=ctmp[0][0:127, :], in0=kx1, in1=ccos[0:127, :], op=ALU.mult), reads=rd, writes=[("ctmp", 0)])
                        op("vector", lambda e: e.tensor_tensor(out=ctmp[1][0:127, :], in0=kx2, in1=csin[0:127, :], op=ALU.mult), reads=rd, writes=[("ctmp", 1)])
                        op("vector", lambda e: e.tensor_tensor(out=ctmp[2][0:127, :], in0=kx1, in1=csin[0:127, :], op=ALU.mult), reads=rd, writes=[("ctmp", 2)])
                        op("vector", lambda e: e.tensor_tensor(out=ctmp[3][0:127, :], in0=kx2, in1=ccos[0:127, :], op=ALU.mult), reads=rd, writes=[("ctmp", 3)])
                        op("vector", lambda e: e.tensor_copy(out=KCs[0:127, 16:64], in_=pA[0][0:127, 16:64]), reads=[("pA", 0)], writes=["KCs"])
                        op("vector", lambda e: e.tensor_tensor(out=KCs[0:127, 0:8], in0=ctmp[0][0:127, :], in1=ctmp[1][0:127, :], op=ALU.subtract),
                           reads=[("ctmp", 0), ("ctmp", 1)], writes=["KCs"])
                        op("vector", lambda e: e.tensor_tensor(out=KCs[0:127, 8:16], in0=ctmp[2][0:127, :], in1=ctmp[3][0:127, :], op=ALU.add),
                           reads=[("ctmp", 2), ("ctmp", 3)], writes=["KCs"])
                        op("tensor", lambda e: e.transpose(out=pT[0][0:64, 0:127], in_=KCs[0:127, 0:64], identity=ident[0:127, 0:127]),
                           reads=["KCs", "const"], writes=[("pT", 0)])
                        op("vector", lambda e: e.tensor_copy(out=KCT[0:64, 0:127], in_=pT[0][0:64, 0:127]), reads=[("pT", 0)], writes=["KCT"])
                    else:
                        op("vector", lambda e: e.tensor_copy(out=VCA[0:127, 0:64], in_=pA[0][0:127, 0:64]), reads=[("pA", 0)], writes=["VCA"])

                if nxt is None and out_pref is not None:
                    out_pref()
                def cmp_chunk(r, qc, po):
                    def fin(u, r=r, qc=qc):
                        ri = cnt["rt"] % 2
                        cnt["rt"] += 1
                        po4 = pO[u.po][:, 0:388].rearrange("p (j c) -> p j c", c=97)
                        op("vector", lambda e: e.tensor_scalar(out=rinv[ri][:].unsqueeze(2), in0=po4[:, :, 64:65], scalar1=1e-30, scalar2=None, op0=ALU.add),
                           reads=[("pO", u.po)], writes=[("rinv", ri)])
                        op("vector", lambda e: e.reciprocal(out=rinv[ri][:], in_=rinv[ri][:]), reads=[("rinv", ri)], writes=[("rinv", ri)])
                        op("vector", lambda e: e.tensor_tensor(out=scl[ri][:], in0=rinv[ri][:], in1=G[:, 4 * qc:4 * qc + 4, 3 * r], op=ALU.mult),
                           reads=[("rinv", ri)] + [("G", 4 * qc + j) for j in range(4)], writes=[("scl", ri)])
                        accres = [("acc", 4 * qc + j) for j in range(4)]
                        impres = [("IMP", 4 * qc + j) for j in range(4)]
                        op("vector", lambda e: e.tensor_tensor(
                            out=acc[:, 4 * qc:4 * qc + 4, r * 64:(r + 1) * 64], in0=po4[:, :, 0:64],
                            in1=scl[ri][:].unsqueeze(2).broadcast_to([128, 4, 64]), op=ALU.mult),
                           reads=[("pO", u.po), ("scl", ri)], writes=accres)
                        IMPv = IMP[:, 4 * qc * 32:(4 * qc + 4) * 32].rearrange("p (j c) -> p j c", c=32)
                        rbc = rinv[ri][:].unsqueeze(2).broadcast_to([128, 4, 32])
                        if r == 0:
                            op("vector", lambda e: e.tensor_tensor(out=IMPv, in0=po4[:, :, 65:97], in1=rbc, op=ALU.mult),
                               reads=[("pO", u.po), ("rinv", ri)], writes=impres)
                        else:
                            it = cnt["it"] % 2
                            cnt["it"] += 1
                            tv = imtmp[it][:].rearrange("p (j c) -> p j c", c=32)
                            op("vector", lambda e: e.tensor_tensor(out=tv, in0=po4[:, :, 65:97], in1=rbc, op=ALU.mult),
                               reads=[("pO", u.po), ("rinv", ri)], writes=[("imtmp", it)])
                            op("gpsimd", lambda e: e.tensor_tensor(out=IMPv, in0=IMPv, in1=tv, op=ALU.add),
                               reads=[("imtmp", it)] + impres, writes=impres)
                        cdone[qc] = cdone.get(qc, 0) + 1
                    u = new_unit(fin, po=po)
                    return {"np": 127, "w": 512, "unit": u, "last": True,
                            "qk": [(0, 512, KCT[0:96, 0:127], QT[0:96, r, qc * 512:(qc + 1) * 512],
                                    (ident[:, 0:127], cmpmask[:, qc * 512:(qc + 1) * 512]),
                                    ["KCT", "const"] + [("QT", r, 4 * qc + j) for j in range(4)])],
                            "pv": [(j, VCA[0:127, 0:97], j * 97, 97, True, True, ["VCA", "Vones", "const"]) for j in range(4)]}

                def select_tile(T):
                    assert cdone.get(T // 4) == 4, "cmp units not finished"
                    sl = slice(T * 32, (T + 1) * 32)
                    op("vector", lambda e: e.tensor_tensor(out=IMP[:, sl], in0=IMP[:, sl], in1=impA[:, sl], op=ALU.mult),
                       reads=[("IMP", T), "const"], writes=[("IMP", T)])
                    op("vector", lambda e: e.tensor_tensor(out=IMP[:, sl], in0=IMP[:, sl], in1=impB[:, sl], op=ALU.add),
                       reads=[("IMP", T), "const"], writes=[("IMP", T)])
                    op("vector", lambda e: e.max(out=M8[:, T, :], in_=IMP[:, sl]), reads=[("IMP", T)], writes=[("M8", T)])
                    op("vector", lambda e: e.tensor_scalar(
                        out=stageQ[:, T, :, 64:96], in0=IMP[:, sl].unsqueeze(1).broadcast_to([128, 4, 32]),
                        scalar1=M8[:, T, 7:8], scalar2=NEGM, op0=ALU.is_lt, op1=ALU.mult),
                       reads=[("IMP", T), ("M8", T)], writes=[("stageQ", T)])

                def select_tr(T):
                    for r in range(4):
                        op("tensor", lambda e, r=r: e.transpose(out=pT[0][0:96, r * 128:(r + 1) * 128], in_=stageQ[:, T, r, 0:96], identity=ident[:]),
                           reads=[("stageQ", T), "const"], writes=[("pT", 0)])
                    op("vector", lambda e: e.tensor_copy(out=QT[64:96, :, T * 128:(T + 1) * 128],
                                                         in_=pT[0][64:96, 0:512].rearrange("p (r t) -> p r t", t=128)),
                       reads=[("pT", 0)], writes=[("QT", r, T) for r in range(4)])

                def super_fin(qb, branch, ndone, po):
                    def fin(u):
                        ri = cnt["rt"] % 2
                        cnt["rt"] += 1
                        fi = cnt["ft"] % 2
                        cnt["ft"] += 1
                        po4 = pO[po][:, 0:260].rearrange("p (r c) -> p r c", c=65)
                        op("vector", lambda e: e.reciprocal(out=rinv[ri][:].unsqueeze(2), in_=po4[:, :, 64:65]),
                           reads=[("pO", po)], writes=[("rinv", ri)])
                        Gv = G[:, qb, :].rearrange("p (r b) -> p r b", b=3)[:, :, branch]
                        op("vector", lambda e: e.tensor_tensor(out=scl[ri][:], in0=rinv[ri][:], in1=Gv, op=ALU.mult),
                           reads=[("rinv", ri), ("G", qb)], writes=[("scl", ri)])
                        op("vector", lambda e: e.tensor_tensor(out=fint[fi][:], in0=po4[:, :, 0:64],
                                                               in1=scl[ri][:].unsqueeze(2).broadcast_to([128, 4, 64]), op=ALU.mult),
                           reads=[("pO", po), ("scl", ri)], writes=[("fint", fi)])
                        av = acc[:, qb, :].rearrange("p (r d) -> p r d", d=64)
                        op("gpsimd", lambda e: e.tensor_tensor(out=av, in0=av, in1=fint[fi][:], op=ALU.add),
                           reads=[("fint", fi), ("acc", qb)], writes=[("acc", qb)])
                        if ndone is not None:
                            ndone(qb)
                    return fin

                cdone = {}
                strm = Streamer()
                for T in range(NT - 1, -1, -1):
                    qb = T
                    chunks = []
                    po = next_po()
                    for r in range(4):
                        if T % 4 == 3:
                            chunks.append(cmp_chunk(r, T // 4, 1 - po))
                        u = new_unit(super_fin(qb, 2, None, po) if r == 3 else nofin, po=po, ooff=65 * r)
                        kbs = list(range(max(0, qb - 4), qb + 1))
                        masks = {qb: (ident[:], cm[:])}
                        if qb - 4 >= 0:
                            masks[qb - 4] = (ident[:], am[:])
                        chunks += blocks_chunks(
                            u, QT[0:128, r, qb * 128:(qb + 1) * 128], ("QT", r, qb),
                            lambda kb: KT[0:128, 1, kb * 128:(kb + 1) * 128], lambda kb: ("KT", 1, kb),
                            lambda kb: Vaug[:, kb, 1, 0:65], lambda kb: ("V", kb),
                            kbs, qb, masks)
                    bg = []
                    if T % 4 == 2:
                        bg = [(lambda tt=tt: select_tile(tt)) for tt in range(4 * (T // 4) + 3, 4 * (T // 4) - 1, -1)]
                    if T % 4 == 1:
                        bg = [(lambda tt=tt: select_tr(tt)) for tt in range(4 * (T // 4) + 3, 4 * (T // 4) - 1, -1)]
                    strm.run(chunks, bg)
                strm.flush()

                tdone = {}

                def tile_done(T, g=g):
                    op("vector", lambda e: e.tensor_tensor(out=zs[:, T, :], in0=zs[:, T, :], in1=acc[:, T, :], op=ALU.mult),
                       reads=[("acc", T), ("zs", T)], writes=[("zs", T)])
                    dst = og_d[T * 128:(T + 1) * 128, 256 * g:256 * g + 256]
                    op("sync", lambda e: e.dma_start(out=dst, in_=zs[:, T, :]), reads=[("zs", T)], writes=[("ogd", T)], dma="ogst")

                def unit_done(qb):
                    tdone[qb] = 4
                    tile_done(qb)

                strm = Streamer(deep=(nxt is None))
                for T in range(NT - 1, -1, -1):
                    qb = T
                    chunks = []
                    po = next_po()
                    for r in range(4):
                        u = new_unit(super_fin(qb, 1, unit_done, po) if r == 3 else nofin, po=po, ooff=65 * r)
                        chunks += blocks_chunks(
                            u, QT[0:128, r, qb * 128:(qb + 1) * 128], ("QT", r, qb),
                            lambda kb: KT[0:128, 0, kb * 128:(kb + 1) * 128], lambda kb: ("KT", 0, kb),
                            lambda kb: Vaug[:, kb, 0, 0:65], lambda kb: ("V", kb),
                            list(range(qb + 1)), qb, {qb: (ident[:], cm[:])})
                    bg = []
                    if nxt is not None:
                        n1, n2, n3 = nxt
                        free = []
                        if T + 1 < NT:
                            free = n1(T + 1)
                            bg.append(lambda T=T: tdone.get(T + 1) == 4 or _raise("tile not released"))
                            bg += n2(T + 1)
                        if T + 2 < NT:
                            bg += n3(T + 2)
                        strm.run(chunks, bg, free)
                    else:
                        strm.run(chunks, bg)
                strm.flush()
                if nxt is not None:
                    n1, n2, n3 = nxt
                    run_all(n3(1) + n1(0) + n2(0) + n3(0))
            chk("nsa")

        xin = din["x"]

        def chk(tag):
            if stop == tag:
                raise _Stop()

        try:
            chk("const")
            if 0 in layers:
                fox_preload()
                phase_norm(0, xin)
                chk("norm")
                fox_layer(lambda: (prefetch_out(0 if 1 in layers else 1, din["fox_w_out"], slot=1),
                                   nsa_preload_g0() if 1 in layers else None))
                chk("fox")
                phase_out(0 if 1 in layers else 1, xin, din["fox_w_out"], True)
            if 1 in layers:
                src = x1_d if 0 in layers else xin
                if 0 not in layers:
                    phase_norm(1, src)
                chk("norm")
                nsa_layer(lambda: prefetch_out(1, din["nsa_w_out"], slot=1))
                chk("nsa")
                phase_out(1, src, din["nsa_w_out"], False)
        except _Stop:
            pass
        P.emit()
    return nc


_CACHE = {}


def kernel(**inputs):
    if "nc" not in _CACHE:
        _CACHE["nc"] = build()
        _CACHE["consts"] = host_constants()
    nc = _CACHE["nc"]
    consts = _CACHE["consts"]
    x = np.ascontiguousarray(np.asarray(inputs["x"], dtype=np.float32))
    B = x.shape[0]
    shared = {}
    for name, shape in IN_SPECS:
        if name == "x":
            continue
        a = np.asarray(inputs[name], dtype=np.float32)
        shared[name] = np.ascontiguousarray(a.reshape(shape))
    shared.update(consts)
    in_maps = []
    for b in range(B):
        m = dict(shared)
        m["x"] = x[b]
        in_maps.append(m)
    res = run_bass_kernel_spmd(nc, in_maps, core_ids=list(range(B)))
    out = np.stack([np.asarray(r["out"], dtype=np.float32) for r in res.results], axis=0)
    return out
```
